# Optimizing a Trainium2 kernel written in Bass

```python
import math
import jax
import jax.numpy as jnp
from jax import lax
import numpy as np

D_MODEL = 1024
BATCH = 4
SEQ = 8192
DEPTH = 2

CHUNK = 64
N_META = 16
META_PAD = CHUNK - N_META
D_FF = 2816
LN_EPS = 1e-5
DEEPNORM_ALPHA = (2.0 * DEPTH) ** 0.25
DEEPNORM_BETA = (8.0 * DEPTH) ** -0.25

GDN_HEADS = 4
GDN_DK = 128
GDN_DV = 128
GDN_CONV = 4
RWKV_HEADS = 4
RWKV_HEAD = 64
RWKV_DECAY_RANK = 32
RWKV_A_RANK = 32
RWKV_GATE_RANK = 64
RWKV_LNX_EPS = 64e-5
RET_HEADS = 4
RET_DK = 32
RET_DV = 64
ROPE_BASE = 10000.0

D_A = GDN_HEADS * GDN_DV
D_B = RWKV_HEADS * RWKV_HEAD
D_C = RET_HEADS * RET_DV
D_MIX = D_A + D_B + D_C
GDN_QKV = 2 * GDN_HEADS * GDN_DK + GDN_HEADS * GDN_DV
D_A_IN = GDN_QKV + D_A + 2 * GDN_HEADS
D_B_IN = 3 * D_B + RWKV_DECAY_RANK + RWKV_A_RANK + RWKV_GATE_RANK
D_C_IN = 2 * RET_HEADS * RET_DK + 2 * D_C
D_IN = D_A_IN + D_B_IN + D_C_IN
GDN_SPLITS = (GDN_QKV, GDN_QKV + D_A, GDN_QKV + D_A + GDN_HEADS)
RWKV_SPLITS = (D_B, 2 * D_B, 3 * D_B, 3 * D_B + RWKV_DECAY_RANK,
               3 * D_B + RWKV_DECAY_RANK + RWKV_A_RANK)
RET_SPLITS = (RET_HEADS * RET_DK, 2 * RET_HEADS * RET_DK, 2 * RET_HEADS * RET_DK + D_C)

kernel_name = 'hybrid_gdn_rwkv7_retnet_macaron_deepnorm'


def layer_norm(x, g, b):
    xf = x.astype(jnp.float32)
    mu = xf.mean(-1, keepdims=True)
    var = jnp.square(xf - mu).mean(-1, keepdims=True)
    return ((xf - mu) * lax.rsqrt(var + LN_EPS)).astype(x.dtype) * g + b


def head_group_norm(y, g, b, eps):
    yf = y.astype(jnp.float32)
    mu = yf.mean(-1, keepdims=True)
    var = jnp.square(yf - mu).mean(-1, keepdims=True)
    yn = (yf - mu) * lax.rsqrt(var + eps)
    return yn.reshape(y.shape[0], y.shape[1], -1) * g + b


def l2_normalize(x):
    xf = x.astype(jnp.float32)
    return xf * lax.rsqrt(jnp.sum(xf * xf, -1, keepdims=True) + 1e-6)


def swiglu(h, w_in, w_out):
    gate, up = jnp.split(h @ w_in, 2, axis=-1)
    return (jax.nn.silu(gate) * up) @ w_out


def causal_depthwise_conv(x, w):
    k_len, ch = w.shape
    return lax.conv_general_dilated(x, w[:, None, :].astype(x.dtype), window_strides=(1,),
                                    padding=[(k_len - 1, 0)],
                                    dimension_numbers=('NWC', 'WIO', 'NWC'),
                                    feature_group_count=ch)


def rotary(x):
    L, d = x.shape[1], x.shape[-1]
    half = d // 2
    inv_freq = 1.0 / (ROPE_BASE ** jnp.linspace(0.0, 1.0, half, dtype=jnp.float32))
    ang = jnp.arange(L, dtype=jnp.float32)[:, None] * inv_freq
    cos, sin = jnp.cos(ang)[None, :, None, :], jnp.sin(ang)[None, :, None, :]
    xf = x.astype(jnp.float32)
    x1, x2 = xf[..., :half], xf[..., half:]
    return jnp.concatenate([x1 * cos - x2 * sin, x1 * sin + x2 * cos], -1)


def to_chunks(x):
    b, L, h = x.shape[:3]
    x = x.reshape(b, L // CHUNK, CHUNK, h, *x.shape[3:])
    return jnp.moveaxis(x, (1, 3), (0, 2))


def from_chunks(x):
    x = jnp.moveaxis(x, (0, 2), (1, 3))
    b, n, c, h = x.shape[:4]
    return x.reshape(b, n * c, h, *x.shape[4:])


def chunked_gated_delta_rule(q, k, v, log_g, beta):
    dk, dv = q.shape[-1], v.shape[-1]
    q = to_chunks(q.astype(jnp.float32)) * dk ** -0.5
    k = to_chunks(k.astype(jnp.float32))
    v = to_chunks(v.astype(jnp.float32))
    log_g = to_chunks(log_g.astype(jnp.float32))
    beta = to_chunks(beta.astype(jnp.float32))
    G = jnp.cumsum(log_g, axis=-1)
    causal = jnp.tril(jnp.ones((CHUNK, CHUNK), bool))
    strict = jnp.tril(jnp.ones((CHUNK, CHUNK), bool), -1)
    diff = G[..., :, None] - G[..., None, :]
    decay = jnp.where(causal, jnp.exp(jnp.where(causal, diff, 0.0)), 0.0)
    k_beta = k * beta[..., None]
    a_mat = jnp.where(strict, jnp.einsum('nbhcd,nbhmd->nbhcm', k_beta, k) * decay, 0.0)
    rhs = jnp.concatenate([v * beta[..., None], k_beta * jnp.exp(G)[..., None]], -1)
    sol = lax.linalg.triangular_solve(a_mat, rhs, left_side=True, lower=True,
                                      unit_diagonal=True)
    u, w = sol[..., :dv], sol[..., dv:]
    qk = jnp.einsum('nbhcd,nbhmd->nbhcm', q, k) * decay
    q_dec = q * jnp.exp(G)[..., None]
    g_last = G[..., -1]
    k_dec = k * jnp.exp(g_last[..., None] - G)[..., None]

    def step(S, xs):
        q_i, qk_i, u_i, w_i, k_i, gl_i = xs
        v_new = u_i - jnp.einsum('bhcd,bhde->bhce', w_i, S)
        o_i = jnp.einsum('bhcd,bhde->bhce', q_i, S) + jnp.einsum('bhcm,bhme->bhce', qk_i, v_new)
        S = S * jnp.exp(gl_i)[..., None, None] + jnp.einsum('bhcd,bhce->bhde', k_i, v_new)
        return S, o_i

    S0 = jnp.zeros(q.shape[1:3] + (dk, dv), jnp.float32)
    _, o = lax.scan(step, S0, (q_dec, qk, u, w, k_dec, g_last))
    return from_chunks(o)


def gated_deltanet(p, conv_w, a_log, dt_bias, norm_w):
    b, L, _ = p.shape
    qkv, z, b_raw, a_raw = jnp.split(p, GDN_SPLITS, axis=-1)
    qkv = jax.nn.silu(causal_depthwise_conv(qkv, conv_w))
    q, k, v = jnp.split(qkv, (GDN_HEADS * GDN_DK, 2 * GDN_HEADS * GDN_DK), axis=-1)
    q = l2_normalize(q.reshape(b, L, GDN_HEADS, GDN_DK))
    k = l2_normalize(k.reshape(b, L, GDN_HEADS, GDN_DK))
    v = v.reshape(b, L, GDN_HEADS, GDN_DV)
    beta = jax.nn.sigmoid(b_raw.astype(jnp.float32))
    log_g = -jnp.exp(a_log) * jax.nn.softplus(a_raw.astype(jnp.float32) + dt_bias)
    o = chunked_gated_delta_rule(q, k, v, log_g, beta)
    o = o * lax.rsqrt(jnp.mean(o * o, -1, keepdims=True) + LN_EPS) * norm_w
    o = o * jax.nn.silu(z.astype(jnp.float32).reshape(b, L, GDN_HEADS, GDN_DV))
    return o.reshape(b, L, D_A).astype(p.dtype)


def rwkv7_scan(r, decay, k, v, a, b):
    xs = tuple(jnp.moveaxis(t.astype(jnp.float32), 1, 0) for t in (r, decay, k, v, a, b))

    def step(S, xt):
        r_t, w_t, k_t, v_t, a_t, b_t = xt
        sa = jnp.einsum('bhvk,bhk->bhv', S, a_t)
        S = S * w_t[:, :, None, :] + sa[..., None] * b_t[:, :, None, :] + v_t[..., None] * k_t[:, :, None, :]
        return S, jnp.einsum('bhvk,bhk->bhv', S, r_t)

    bsz, _, h, n = r.shape
    _, y = lax.scan(step, jnp.zeros((bsz, h, n, n), jnp.float32), xs)
    return jnp.moveaxis(y, 0, 1)


def rwkv7_mixer(p, mu, w0, w_up, a0, a_up, g_up, k_k, k_a, r_k, lnx_g, lnx_b):
    b, L, _ = p.shape
    pf = p.astype(jnp.float32)
    prev = jnp.pad(pf, ((0, 0), (1, 0), (0, 0)))[:, :-1]
    pf = pf + (prev - pf) * mu
    r, k, v, xw, xa, xg = jnp.split(pf, RWKV_SPLITS, axis=-1)
    log_w = -jax.nn.softplus(-(w0 + jnp.tanh(xw) @ w_up)) - 0.5
    decay = jnp.exp(-jnp.exp(log_w))
    a = jax.nn.sigmoid(a0 + xa @ a_up)
    g = jax.nn.sigmoid(xg) @ g_up
    heads = lambda t: t.reshape(b, L, RWKV_HEADS, RWKV_HEAD)
    kk = l2_normalize(heads(k * k_k))
    k = k * (1.0 + (a - 1.0) * k_a)
    r_h, k_h, v_h, a_h = heads(r), heads(k), heads(v), heads(a)
    y = rwkv7_scan(r_h, heads(decay), k_h, v_h, -kk, kk * a_h)
    y = head_group_norm(y, lnx_g, lnx_b, RWKV_LNX_EPS)
    y = y + (jnp.sum(r_h * k_h * r_k, -1, keepdims=True) * v_h).reshape(b, L, D_B)
    return (y * g).astype(p.dtype)


def chunked_retention(q, k, v):
    dk = q.shape[-1]
    q = to_chunks(q.astype(jnp.float32))
    k = to_chunks(k.astype(jnp.float32)) * dk ** -0.5
    v = to_chunks(v.astype(jnp.float32))
    log_gamma = jnp.log(1.0 - 2.0 ** (-5.0 - jnp.arange(RET_HEADS, dtype=jnp.float32)))
    idx = jnp.arange(CHUNK, dtype=jnp.float32)
    diff = idx[:, None] - idx[None, :]
    d_intra = jnp.where(diff >= 0, jnp.exp(log_gamma[:, None, None] * jnp.maximum(diff, 0.0)), 0.0)
    intra = jnp.einsum('nbhcm,nbhme->nbhce', jnp.einsum('nbhcd,nbhmd->nbhcm', q, k) * d_intra, v)
    q_dec = q * jnp.exp(log_gamma[:, None] * (idx + 1.0))[..., None]
    k_dec = k * jnp.exp(log_gamma[:, None] * (CHUNK - 1.0 - idx))[..., None]
    chunk_decay = jnp.exp(log_gamma * CHUNK)[:, None, None]

    def step(S, xs):
        q_i, k_i, v_i = xs
        o_i = jnp.einsum('bhcd,bhde->bhce', q_i, S)
        S = S * chunk_decay + jnp.einsum('bhcd,bhce->bhde', k_i, v_i)
        return S, o_i

    S0 = jnp.zeros(q.shape[1:3] + (dk, v.shape[-1]), jnp.float32)
    _, cross = lax.scan(step, S0, (q_dec, k_dec, v))
    return from_chunks(intra + cross)


def retention_mixer(p, norm_g, norm_b):
    b, L, _ = p.shape
    q, k, v, g = jnp.split(p, RET_SPLITS, axis=-1)
    q = rotary(q.reshape(b, L, RET_HEADS, RET_DK))
    k = rotary(k.reshape(b, L, RET_HEADS, RET_DK))
    y = chunked_retention(q, k, v.reshape(b, L, RET_HEADS, RET_DV))
    y = head_group_norm(y, norm_g, norm_b, LN_EPS) * jax.nn.silu(g.astype(jnp.float32))
    return y.astype(p.dtype)


def hybrid_mixer(h, w_in, w_out, gdn_conv_w, gdn_a_log, gdn_dt_bias, gdn_norm_w,
                 rwkv_mu, rwkv_w0, rwkv_w_up, rwkv_a0, rwkv_a_up, rwkv_g_up, rwkv_k_k,
                 rwkv_k_a, rwkv_r_k, rwkv_lnx_g, rwkv_lnx_b, ret_norm_g, ret_norm_b):
    proj = h @ w_in
    pa, pb, pc = jnp.split(proj, (D_A_IN, D_A_IN + D_B_IN), axis=-1)
    ya = gated_deltanet(pa, gdn_conv_w, gdn_a_log, gdn_dt_bias, gdn_norm_w)
    yb = rwkv7_mixer(pb, rwkv_mu, rwkv_w0, rwkv_w_up, rwkv_a0, rwkv_a_up, rwkv_g_up,
                     rwkv_k_k, rwkv_k_a, rwkv_r_k, rwkv_lnx_g, rwkv_lnx_b)
    yc = retention_mixer(pc, ret_norm_g, ret_norm_b)
    return jnp.concatenate([ya, yb, yc], axis=-1) @ w_out


def setup_inputs(seed: int = 0) -> dict:
    key = jax.random.key(seed)
    ks = jax.random.split(key, 28)
    f32 = jnp.float32
    nrm = lambda k, shape, scale: jax.random.normal(k, shape, f32) * scale
    dt = jnp.exp(jax.random.uniform(ks[12], (DEPTH, GDN_HEADS), f32, math.log(1e-3), math.log(1e-1)))
    return {
        'x': nrm(ks[0], (BATCH, SEQ, D_MODEL), 1.0),
        'meta_tokens': nrm(ks[1], (N_META, D_MODEL), 1.0),
        'ln_g': 1.0 + nrm(ks[2], (DEPTH, 3, D_MODEL), 0.02),
        'ln_b': nrm(ks[3], (DEPTH, 3, D_MODEL), 0.02),
        'w_ff1_in': nrm(ks[4], (DEPTH, D_MODEL, 2 * D_FF), D_MODEL ** -0.5),
        'w_ff1_out': nrm(ks[5], (DEPTH, D_FF, D_MODEL), DEEPNORM_BETA * D_FF ** -0.5),
        'w_ff2_in': nrm(ks[6], (DEPTH, D_MODEL, 2 * D_FF), D_MODEL ** -0.5),
        'w_ff2_out': nrm(ks[7], (DEPTH, D_FF, D_MODEL), DEEPNORM_BETA * D_FF ** -0.5),
        'w_in': nrm(ks[8], (DEPTH, D_MODEL, D_IN), D_MODEL ** -0.5),
        'w_out': nrm(ks[9], (DEPTH, D_MIX, D_MODEL), DEEPNORM_BETA * D_MIX ** -0.5),
        'gdn_conv_w': nrm(ks[10], (DEPTH, GDN_CONV, GDN_QKV), GDN_CONV ** -0.5),
        'gdn_a_log': jnp.log(jax.random.uniform(ks[11], (DEPTH, GDN_HEADS), f32, 1.0, 16.0)),
        'gdn_dt_bias': dt + jnp.log(-jnp.expm1(-dt)),
        'gdn_norm_w': 1.0 + nrm(ks[13], (DEPTH, GDN_DV), 0.02),
        'rwkv_mu': jax.random.uniform(ks[14], (DEPTH, D_B_IN), f32),
        'rwkv_w0': jax.random.uniform(ks[15], (DEPTH, D_B), f32, -6.5, -1.0),
        'rwkv_w_up': nrm(ks[16], (DEPTH, RWKV_DECAY_RANK, D_B), RWKV_DECAY_RANK ** -0.5),
        'rwkv_a0': nrm(ks[17], (DEPTH, D_B), 0.1),
        'rwkv_a_up': nrm(ks[18], (DEPTH, RWKV_A_RANK, D_B), RWKV_A_RANK ** -0.5),
        'rwkv_g_up': nrm(ks[19], (DEPTH, RWKV_GATE_RANK, D_B), RWKV_GATE_RANK ** -0.5),
        'rwkv_k_k': 0.85 + nrm(ks[20], (DEPTH, D_B), 0.02),
        'rwkv_k_a': 1.0 + nrm(ks[21], (DEPTH, D_B), 0.02),
        'rwkv_r_k': nrm(ks[22], (DEPTH, RWKV_HEADS, RWKV_HEAD), 0.1),
        'rwkv_lnx_g': 1.0 + nrm(ks[23], (DEPTH, D_B), 0.02),
        'rwkv_lnx_b': nrm(ks[24], (DEPTH, D_B), 0.02),
        'ret_norm_g': 1.0 + nrm(ks[25], (DEPTH, D_C), 0.02),
        'ret_norm_b': nrm(ks[26], (DEPTH, D_C), 0.02),
    }


def reference(x, meta_tokens, ln_g, ln_b, w_ff1_in, w_ff1_out, w_ff2_in, w_ff2_out, w_in, w_out,
              gdn_conv_w, gdn_a_log, gdn_dt_bias, gdn_norm_w, rwkv_mu, rwkv_w0, rwkv_w_up,
              rwkv_a0, rwkv_a_up, rwkv_g_up, rwkv_k_k, rwkv_k_a, rwkv_r_k, rwkv_lnx_g, rwkv_lnx_b,
              ret_norm_g, ret_norm_b):
    bsz, _, d = x.shape
    pad = jnp.zeros((bsz, META_PAD, d), x.dtype)
    meta = jnp.broadcast_to(meta_tokens.astype(x.dtype)[None], (bsz, N_META, d))
    h = jnp.concatenate([pad, meta, x], axis=1)
    L = h.shape[1]
    valid = (jnp.arange(L) >= META_PAD).astype(h.dtype)[None, :, None]
    for l in range(DEPTH):
        h = layer_norm(DEEPNORM_ALPHA * h + 0.5 * swiglu(h, w_ff1_in[l], w_ff1_out[l]), ln_g[l, 0], ln_b[l, 0])
        mix = hybrid_mixer(h * valid, w_in[l], w_out[l], gdn_conv_w[l], gdn_a_log[l], gdn_dt_bias[l],
                           gdn_norm_w[l], rwkv_mu[l], rwkv_w0[l], rwkv_w_up[l], rwkv_a0[l], rwkv_a_up[l],
                           rwkv_g_up[l], rwkv_k_k[l], rwkv_k_a[l], rwkv_r_k[l], rwkv_lnx_g[l],
                           rwkv_lnx_b[l], ret_norm_g[l], ret_norm_b[l])
        h = layer_norm(DEEPNORM_ALPHA * h + mix, ln_g[l, 1], ln_b[l, 1])
        h = layer_norm(DEEPNORM_ALPHA * h + 0.5 * swiglu(h, w_ff2_in[l], w_ff2_out[l]), ln_g[l, 2], ln_b[l, 2])
    return h[:, CHUNK:]
```

```python
import numpy as np
import concourse.bass as bass
import concourse.mybir as mybir
from concourse.bass_utils import run_bass_kernel_spmd

F32 = mybir.dt.float32
BF16 = mybir.dt.bfloat16
AF = mybir.ActivationFunctionType
ALU = mybir.AluOpType


class Res:
    __slots__ = ("name", "w", "rd")

    def __init__(self, name):
        self.name = name
        self.w = None
        self.rd = {}


class T:
    def __init__(self, ap, res):
        self.ap = ap
        self.res = res

    def __getitem__(self, k):
        return self.ap[k]


class Prog:
    ENGS = ["pe", "act", "dve", "pool", "sp"]

    def __init__(self, nc, a32=53000, a16=0):
        self.nc = nc
        self.ops = {e: [] for e in self.ENGS}
        self.cnt = {e: 0 for e in self.ENGS}
        self.seen = {e: {} for e in self.ENGS}
        self.slots = {}
        self.ctx = []
        self.nalloc = 0
        self.a32, self.a16 = a32, a16
        cm = nc.sbuf_tensor("A32", [128, a32], F32); self.A32 = cm.__enter__(); self.ctx.append(cm)
        self.o32 = self.o16 = 0
        self.dynsel = None
        self.dynval = None
        self.banks = [self.ps([128, 512], F32, f"bank{i}") for i in range(8)]
        self.bi = 0

    def psum(self):
        b = self.banks[self.bi % 8]
        self.bi += 1
        return b

    def reset(self):
        self.o32 = self.o16 = 0

    def barrier(self):
        targets = [(e, self.cnt[e]) for e in self.ENGS[:4] if self.cnt[e] > 0] + list(self.slots.items())
        for e in self.ENGS:
            waits = [(k, v) for k, v in targets if self.seen[e].get(k, 0) < v]
            for k, v in waits:
                self.seen[e][k] = v
            if waits:
                self.ops[e].append((waits, None, None))

    def sb(self, shape, dt=F32, name=None):
        self.nalloc += 1
        name = name or f"t{self.nalloc}"
        parts = shape[0]
        n = 1
        for s_ in shape[1:]:
            n *= s_
        if dt == F32:
            off = self.o32; self.o32 += n
            assert self.o32 <= self.a32, ("A32 overflow", name, self.o32)
            ap = self.A32[0:parts, off:off + n]
        else:
            n2 = (n + 1) // 2
            off = self.o32; self.o32 += n2
            assert self.o32 <= self.a32, ("A32 overflow", name, self.o32)
            ap = self.A32[0:parts, off:off + n2].bitcast(dt)
            assert tuple(ap.shape) == (parts, n2 * 2), ap.shape
            ap = ap[:, 0:n]
        if len(shape) == 3:
            ap = ap.rearrange("p (a b) -> p a b", a=shape[1])
        elif len(shape) == 4:
            ap = ap.rearrange("p (a b c) -> p a b c", a=shape[1], b=shape[2])
        return T(ap, Res(name))

    def ps(self, shape, dt=F32, name=None):
        self.nalloc += 1
        name = name or f"p{self.nalloc}"
        cm = self.nc.psum_tensor(name, list(shape), dt)
        t = cm.__enter__()
        self.ctx.append(cm)
        return T(t, Res(name))

    def dram(self, name, shape, dt, kind):
        t = self.nc.dram_tensor(name, list(shape), dt, kind=kind)
        return T(t.ap(), Res(name))

    def op(self, eng, fn, reads=(), writes=(), dma=None):
        waits = {}
        seen = self.seen[eng]

        def need(tok, raw):
            semkey, val, weng = tok
            if weng == eng and semkey == eng:
                if eng == "pe" or not raw:
                    return
                if dma is None and False:
                    return
            if seen.get(semkey, 0) >= val:
                return
            waits[semkey] = max(waits.get(semkey, 0), val)

        def flat(lst):
            o = []
            for r in lst:
                r = r.res if isinstance(r, T) else r
                if isinstance(r, (list, tuple)):
                    o.extend(r)
                else:
                    o.append(r)
            return o
        reads = flat(reads)
        writes = flat(writes)
        for r in reads:
            if r.w is not None:
                need(r.w, True)
        for r in writes:
            for tok in r.rd.values():
                need(tok, dma is not None)
            if r.w is not None:
                need(r.w, dma is not None)
        for k, v in waits.items():
            seen[k] = v
        if dma is not None:
            self.slots[dma] = self.slots.get(dma, 0) + 16
            tok = (dma, self.slots[dma], eng)
            inc = (dma, 16)
        else:
            self.cnt[eng] += 1
            tok = (eng, self.cnt[eng], eng)
            inc = (eng, 1)
        for r in writes:
            r.w = tok
            r.rd = {}
        for r in reads:
            old = r.rd.get(tok[0])
            if old is None or old[1] < tok[1]:
                r.rd[tok[0]] = tok
        self.ops[eng].append((list(waits.items()), fn, inc))

    def mm(self, out, lhsT, rhs, start=True, stop=True, reads=(), writes=None, **kw):
        o_ap, o_r = out
        l_ap, l_r = lhsT
        r_ap, r_r = rhs
        self.op("pe", lambda e: e.matmul(o_ap, l_ap, r_ap, start=start, stop=stop, **kw),
                reads=[l_r, r_r], writes=[o_r])

    def dma(self, eng, out, in_, slot=None, **kw):
        o_ap, o_r = out
        i_ap, i_r = in_
        if callable(i_ap):
            assert eng == "sp"
            fn_ap = i_ap
            if slot is None:
                rr_ = o_r.res if isinstance(o_r, T) else o_r
                if isinstance(rr_, (list, tuple)):
                    rr_ = rr_[0]
                slot = "d_" + rr_.name
            self.op(eng, lambda e: e.dma_start(out=o_ap, in_=fn_ap(self.dynval), **kw), reads=[i_r], writes=[o_r], dma=slot)
            return
        if slot is None:
            rr_ = o_r.res if isinstance(o_r, T) else o_r
            if isinstance(rr_, (list, tuple)):
                rr_ = rr_[0]
            slot = "d_" + rr_.name
        self.op(eng, lambda e: e.dma_start(out=o_ap, in_=i_ap, **kw), reads=[i_r], writes=[o_r], dma=slot)

    def emit(self):
        nc = self.nc
        semnames = list(self.ENGS[:4]) + list(self.slots.keys())
        sems = {}
        for n in semnames:
            cm = nc.semaphore("s_" + n)
            sems[n] = cm.__enter__()
            self.ctx.append(cm)
        fin = [(k, v) for k, v in self.slots.items()] + [(e, self.cnt[e]) for e in self.ENGS[:4] if self.cnt[e] > 0]
        prog = self

        def run(engname, e):
            if engname == "sp" and prog.dynsel is not None:
                e.reg_load(dreg, prog.dynsel)
                prog.dynval = e.snap(dreg)
            for waits, fn, inc in prog.ops[engname]:
                for k, v in waits:
                    e.wait_ge(sems[k], v)
                if fn is None:
                    continue
                ins = fn(e)
                ins.then_inc(sems[inc[0]], inc[1])
            if engname == "sp":
                for k, v in fin:
                    e.wait_ge(sems[k], v)

        dreg = None
        if prog.dynsel is not None:
            cm = nc.sync.register("dynsel_r")
            dreg = cm.__enter__()
            self.ctx.append(cm)
        with nc.Block() as block:
            @block.tensor
            def _(e):
                run("pe", e)

            @block.scalar
            def _(e):
                run("act", e)

            @block.vector
            def _(e):
                run("dve", e)

            @block.gpsimd
            def _(e):
                run("pool", e)

            @block.sync
            def _(e):
                run("sp", e)
        for cm in reversed(self.ctx):
            cm.__exit__(None, None, None)


ALPHA = 4.0 ** 0.25
LN_EPS = 1e-5
D_FF = 2816
NPROJ = 4104


class TState:
    pass


class TCtx:
    pass


def emit_T(P, d, stages, tiles):
    st = TState()
    NC_ = 2
    cx = []
    for i in range(NC_):
        c = TCtx()
        c.hT = P.sb([128, 8, 512], F32, f"hT{i}")
        c.hb = P.sb([128, 8, 512], BF16, f"hb{i}")
        c.r = P.sb([128, 8, 512], F32, f"r{i}")
        c.act = P.sb([128, 22, 512], BF16, f"act{i}")
        cx.append(c)
    st.hbm = P.sb([128, 8, 64], BF16, "hbm")
    st.ystg = P.sb([128, 8, 512], F32, "ystg")
    st.rsqb = P.sb([128, 8, 512], BF16, "rsqb")
    st.rb = P.sb([128, 8, 512], BF16, "rb")
    st.onesb = P.sb([128, 128], BF16, "onesb")
    NB = 3
    st.wbf = [P.sb([128, 4096], BF16, f"wbf{i}") for i in range(NB)]
    st.wi = 0
    st.sg = [P.sb([128, 512], F32, f"sg{i}") for i in range(2)]
    st.sgi = 0
    st.pout = [P.sb([128, 512], F32, f"pout{i}") for i in range(3)]
    st.pi = 0
    st.mean = P.sb([128, 512], F32, "mean")
    st.msq = P.sb([128, 512], F32, "msq")
    st.var = P.sb([128, 512], F32, "var")
    st.rstd = P.sb([128, 512], F32, "rstd")
    st.lnp = P.sb([128, 48], F32, "lnp_s")
    st.ones = P.sb([128, 128], F32, "ones")
    psum = P.psum

    P.dma("sp", (st.lnp[:], st.lnp), (d['lnp'][:], d['lnp']))
    P.op("pool", lambda e: e.memset(st.ones[:], 1.0), writes=[st.ones])
    P.op("pool", lambda e: e.tensor_copy(out=st.onesb[:], in_=st.ones[:]), reads=[st.ones], writes=[st.onesb])

    def load_wb(wt, panel, nk, npart, ncols):
        i = st.wi % NB
        st.wi += 1
        wb = st.wbf[i]
        tot = nk * npart * ncols
        P.dma("sp", (wb[:, :tot], wb), (wt[panel], wt))
        return wb[:, :tot].rearrange("p (k a c) -> p k a c", k=nk, a=npart), wb

    def layernorm(c, li):
        hT, n, r, hb = c.hT, c.n, c.r, c.hb
        eps = LN_EPS / ALPHA ** 2
        g = st.lnp[:, li * 16: li * 16 + 8]
        b = st.lnp[:, li * 16 + 8: li * 16 + 16]
        rsq, rb = st.rsqb, st.rb
        P.op("act", lambda e: e.activation(out=rsq[:, :, :n], in_=r[:, :, :n], func=AF.Square), reads=[r], writes=[rsq])
        P.op("dve", lambda e: e.tensor_copy(out=rb[:, :, :n], in_=r[:, :, :n]), reads=[r], writes=[rb])
        ps1 = psum(); ps2 = psum()
        for dc in range(8):
            P.mm((ps1[:, :n], ps1), (st.onesb[:], st.onesb), (rb[:, dc, :n], rb), start=dc == 0, stop=dc == 7)
        for dc in range(8):
            P.mm((ps2[:, :n], ps2), (st.onesb[:], st.onesb), (rsq[:, dc, :n], rsq), start=dc == 0, stop=dc == 7)
        mean, msq, var, rstd = st.mean, st.msq, st.var, st.rstd
        P.op("act", lambda e: e.activation(out=mean[:, :n], in_=ps1[:, :n], func=AF.Copy, scale=1.0 / 1024), reads=[ps1], writes=[mean])
        P.op("dve", lambda e: e.tensor_tensor(out=msq[:, :n], in0=mean[:, :n], in1=mean[:, :n], op=ALU.mult), reads=[mean], writes=[msq])
        P.op("dve", lambda e: e.scalar_tensor_tensor(out=var[:, :n], in0=ps2[:, :n], scalar=1.0 / 1024, op0=ALU.mult, in1=msq[:, :n], op1=ALU.subtract),
             reads=[ps2, msq], writes=[var])
        P.op("dve", lambda e: e.tensor_scalar(out=var[:, :n], in0=var[:, :n], scalar1=eps, scalar2=None, op0=ALU.add), reads=[var], writes=[var])
        P.op("act", lambda e: e.activation(out=var[:, :n], in_=var[:, :n], func=AF.Ln), reads=[var], writes=[var])
        P.op("act", lambda e: e.activation(out=rstd[:, :n], in_=var[:, :n], func=AF.Exp, scale=-0.5), reads=[var], writes=[rstd])
        P.op("dve", lambda e: e.tensor_tensor(out=r[:, :, :n], in0=r[:, :, :n], in1=mean[:, :n].unsqueeze(1).to_broadcast([128, 8, n]), op=ALU.subtract),
             reads=[r, mean], writes=[r])
        P.op("dve", lambda e: e.tensor_tensor(out=r[:, :, :n], in0=r[:, :, :n], in1=rstd[:, :n].unsqueeze(1).to_broadcast([128, 8, n]), op=ALU.mult),
             reads=[r, rstd], writes=[r])
        for dc in range(8):
            P.op("act", lambda e, dc=dc: e.activation(out=r[:, dc, :n], in_=r[:, dc, :n], func=AF.Identity, scale=g[:, dc:dc + 1], bias=b[:, dc:dc + 1]),
                 reads=[r, st.lnp], writes=[r])
        P.op("dve", lambda e: e.tensor_copy(out=hb[:, :, :n], in_=r[:, :, :n]), reads=[r], writes=[hb])
        c.hT, c.r = c.r, c.hT

    def ffn(cs, wi, wo, li):
        for pp in range(11):
            wv, wb = load_wb(wi, pp, 8, 2, 256)
            for j in range(2):
                for c in cs:
                    n = c.n
                    pg = psum(); pu = psum()
                    for kc in range(8):
                        P.mm((pg[:, :n], pg), (wv[:, kc, 0, j * 128:(j + 1) * 128], wb), (c.hb[:, kc, :n], c.hb), start=kc == 0, stop=kc == 7)
                    for kc in range(8):
                        P.mm((pu[:, :n], pu), (wv[:, kc, 1, j * 128:(j + 1) * 128], wb), (c.hb[:, kc, :n], c.hb), start=kc == 0, stop=kc == 7)
                    sg = st.sg[st.sgi % 2]; st.sgi += 1
                    P.op("act", lambda e, sg=sg, pg=pg, n=n: e.activation(out=sg[:, :n], in_=pg[:, :n], func=AF.Silu), reads=[pg], writes=[sg])
                    fc = 2 * pp + j
                    P.op("dve", lambda e, sg=sg, pu=pu, fc=fc, c=c, n=n: e.tensor_tensor(out=c.act[:, fc, :n], in0=sg[:, :n], in1=pu[:, :n], op=ALU.mult),
                         reads=[sg, pu], writes=[c.act])
        for dp in range(8):
            wv, wb = load_wb(wo, dp, 22, 1, 128)
            for c in cs:
                n = c.n
                po = psum()
                for kc in range(22):
                    P.mm((po[:, :n], po), (wv[:, kc, 0, :], wb), (c.act[:, kc, :n], c.act), start=kc == 0, stop=kc == 21)
                P.op("dve", lambda e, po=po, dp=dp, r_=c.r, h_=c.hT, n=n: e.scalar_tensor_tensor(out=r_[:, dp, :n], in0=po[:, :n], scalar=0.5 / ALPHA, op0=ALU.mult,
                                                                                          in1=h_[:, dp, :n], op1=ALU.add), reads=[po, c.hT], writes=[c.r])
        for c in cs:
            layernorm(c, li)

    def mixout(cs, li):
        for c in cs:
            n = c.n
            for kc in range(8):
                P.dma("sp", (st.ystg[:, kc, :n], st.ystg), d['y_src'](kc, c.c0, n))
            P.op("dve", lambda e, c=c, n=n: e.tensor_copy(out=c.act[:, 0:8, :n], in_=st.ystg[:, :, :n]), reads=[st.ystg], writes=[c.act])
        for dp in range(8):
            wv, wb = load_wb(d['w_out'], dp, 8, 1, 128)
            for c in cs:
                n = c.n
                po = psum()
                for kc in range(8):
                    P.mm((po[:, :n], po), (wv[:, kc, 0, :], wb), (c.act[:, kc, :n], c.act), start=kc == 0, stop=kc == 7)
                P.op("dve", lambda e, po=po, dp=dp, r_=c.r, h_=c.hT, n=n: e.scalar_tensor_tensor(out=r_[:, dp, :n], in0=po[:, :n], scalar=1.0 / ALPHA, op0=ALU.mult,
                                                                                          in1=h_[:, dp, :n], op1=ALU.add), reads=[po, c.hT], writes=[c.r])
        for c in cs:
            layernorm(c, li)

    def proj(cs):
        for c in cs:
            if c.chunk0:
                P.op("pool", lambda e, c=c: e.tensor_copy(out=st.hbm[:, :, :], in_=c.hb[:, :, :64]), reads=[c.hb], writes=[st.hbm])
                P.op("pool", lambda e: e.memset(st.hbm[:, :, 0:48], 0.0), writes=[st.hbm])

        def rhs(c, kc):
            if c.chunk0:
                return (st.hbm[:, kc, :c.n], st.hbm)
            return (c.hb[:, kc, :c.n], c.hb)
        for pp in range(16):
            wv, wb = load_wb(d['wp'], pp, 8, 1, 256)
            for j in range(2):
                for c in cs:
                    n = c.n
                    po = psum()
                    for kc in range(8):
                        P.mm((po[:, :n], po), (wv[:, kc, 0, j * 128:(j + 1) * 128], wb), rhs(c, kc), start=kc == 0, stop=kc == 7)
                    pt = st.pout[st.pi % 3]; st.pi += 1
                    P.op("act", lambda e, pt=pt, po=po, n=n: e.activation(out=pt[:, :n], in_=po[:, :n], func=AF.Copy), reads=[po], writes=[pt])
                    row = (2 * pp + j) * 128
                    P.dma("pool", d['pm_dst'](row, c.c0, n), (pt[:, :n], pt), slot="o_" + pt.res.name)
        wv, wb = load_wb(d['wpt'], 0, 8, 1, 8)
        for c in cs:
            n = c.n
            po = psum()
            for kc in range(8):
                P.mm((po[0:8, :n], po), (wv[:, kc, 0, :], wb), rhs(c, kc), start=kc == 0, stop=kc == 7)
            pt = st.pout[st.pi % 3]; st.pi += 1
            P.op("act", lambda e, pt=pt, po=po, n=n: e.activation(out=pt[0:8, :n], in_=po[0:8, :n], func=AF.Copy), reads=[po], writes=[pt])
            P.dma("pool", d['psc_dst'](c.c0, n), (pt[0:8, :n], pt), slot="o_" + pt.res.name)

    hv = d['hin'][:].rearrange("(k p) t -> p k t", p=128)
    ov = d['hout'][:].rearrange("(k p) t -> p k t", p=128)
    for t0 in range(0, len(tiles), NC_):
        cs = []
        for i, (c0, n, chunk0) in enumerate(tiles[t0:t0 + NC_]):
            c = cx[i]
            c.c0, c.n, c.chunk0 = c0, n, chunk0
            c.hT, c.r = c.r, c.hT
            P.dma("sp", (c.hT[:, :, :n], c.hT), (hv[:, :, c0:c0 + n], d['hin']))
            if stages[0] != 'mix':
                P.op("dve", lambda e, hb_=c.hb, h_=c.hT, n=n: e.tensor_copy(out=hb_[:, :, :n], in_=h_[:, :, :n]), reads=[c.hT], writes=[c.hb])
            cs.append(c)
        for s in stages:
            if s == 'mix':
                mixout(cs, 1)
            elif s == 'ffn2':
                ffn(cs, d['ffn2_i'], d['ffn2_o'], 2)
            elif s == 'ffn1':
                ffn(cs, d['ffn1_i'], d['ffn1_o'], 0)
            elif s == 'proj':
                proj(cs)
        for c in cs:
            P.dma("pool", (ov[:, :, c.c0:c.c0 + c.n], d['hout']), (c.hT[:, :, :c.n], c.hT), slot="o_" + c.hT.res.name)


W_KINDS = {
    'ffn_in': (11, 8, 2, 256), 'ffn_out': (8, 22, 1, 128), 'proj': (16, 8, 1, 256), 'projt': (1, 8, 1, 8), 'mix': (8, 8, 1, 128)}


def emit_W(P, jobs):
    NB = 4
    stg = [P.sb([128, 4096], F32, f"wstg{i}") for i in range(NB)]
    wbf = [P.sb([128, 4096], BF16, f"wcv{i}") for i in range(NB)]
    cnt = 0
    engs = ["pool", "act", "dve"]
    for src, dst, kind in jobs:
        npan, nk, npart, ncols = W_KINDS[kind]
        tot = nk * npart * ncols
        sv = src[:].rearrange("(k p) f -> p k f", p=128)
        for pn in range(npan):
            s32, s16 = stg[cnt % NB], wbf[cnt % NB]
            v = s32[:, :tot].rearrange("p (k a c) -> p k a c", k=nk, a=npart)
            if kind == 'ffn_in':
                colsl = [slice(pn * 256, (pn + 1) * 256), slice(D_FF + pn * 256, D_FF + (pn + 1) * 256)]
            elif kind == 'projt':
                colsl = [slice(4096, 4104)]
            else:
                colsl = [slice(pn * ncols, (pn + 1) * ncols)]
            for a, cs in enumerate(colsl):
                P.dma("sp", (v[:, :, a, :], s32), (sv[:, :, cs], src))
            eng = engs[cnt % 3]
            if eng == "act":
                P.op("act", lambda e, s16=s16, s32=s32, tot=tot: e.activation(out=s16[:, :tot], in_=s32[:, :tot], func=AF.Copy), reads=[s32], writes=[s16])
            else:
                P.op(eng, lambda e, s16=s16, s32=s32, tot=tot: e.tensor_copy(out=s16[:, :tot], in_=s32[:, :tot]), reads=[s32], writes=[s16])
            P.dma("pool", (dst[pn], dst), (s16[:, :tot], s16), slot="o_" + s16.res.name)
            cnt += 1


L = 8256
NCH = 129


DBG = False


def emit_M(P, d, tiles, yrows, do=('gdn', 'rwkv', 'ret'), ltot=L):
    nchtot = ltot // 64

    mp = P.sb([128, 64], F32, "mp_s")
    cm = P.sb([128, 13, 128], F32, "cm_s")
    P.dma("sp", (mp[:], mp), (d['mp'][:], d['mp']))
    P.dma("sp", (cm[:], cm), (d['cm'][:], d['cm']))
    I128 = cm[:, 0, :]
    I64 = cm[0:64, 0, 0:64]
    UT = cm[0:64, 1, 0:64]
    SLm = cm[0:64, 2, 0:64]
    SU = cm[0:64, 3, 0:64]
    ONES = cm[:, 4, :]
    BLK = cm[:, 5, :]
    LOWUP = cm[:, 6, :]
    QDT = cm[0:64, 8, 0:64]; KDT = cm[0:64, 8, 64:128]
    HMc = [cm[0:64, 9, :], cm[0:64, 10, :]]
    HMr = [cm[0:64, 11, 0:64], cm[0:64, 12, 0:64]]
    DTj = [cm[0:64, 7, 0:64], cm[0:64, 7, 64:128]]

    NF, NQ, NR = 32, 26, 8
    FPt = P.sb([128, NF * 512], F32, "FP"); fres = [Res(f"F{i}") for i in range(NF)]
    QPt = P.sb([64, NQ * 512], F32, "QP"); qres = [Res(f"Q{i}") for i in range(NQ)]
    RPt = P.sb([64, NR * 1024], F32, "RP"); rres = [Res(f"R{i}") for i in range(NR)]

    def Fv(i, k=1, shape=None, parts=128):
        ap = FPt.ap[0:parts, i * 512:(i + k) * 512]
        if shape is not None:
            a, b_ = shape
            ap = ap[:, 0:a * b_].rearrange("p (a b) -> p a b", a=a)
        elif k > 1:
            ap = ap.rearrange("p (a b) -> p a b", a=k)
        return T(ap, fres[i:i + k])

    def Qv(i):
        return T(QPt.ap[:, i * 512:(i + 1) * 512].rearrange("p (c f) -> p c f", f=64), [qres[i]])

    def Rv(i):
        return T(RPt.ap[:, i * 1024:(i + 1) * 1024].rearrange("p (c f) -> p c f", f=128), [rres[i]])


    dbgs = {}

    def dbg(name, ap, t):
        shape = list(ap.shape)
        dd = P.dram("dbg_" + name, shape, F32, "ExternalOutput")
        P.dma("pool", (dd[:], dd), (ap, t), slot="dbg_" + name)
        dbgs[name] = dd

    psum = P.psum

    def run_rr(gens):
        gens = list(gens)
        while gens:
            for gn in list(gens):
                try:
                    next(gn)
                except StopIteration:
                    gens.remove(gn)

    def ld(dst_ap, dst_T, row, nrows, c0, n, halo=0):
        P.dma("sp", (dst_ap[:, halo:halo + n], dst_T), d['src'](row, nrows, c0, n))
        if halo:
            if c0 == 0:
                P.op("pool", lambda e: e.memset(dst_ap[:, 0:halo], 0.0), writes=[dst_T])
            else:
                P.dma("sp", (dst_ap[:, 0:halo], dst_T), d['src'](row, nrows, c0 - halo, halo), allow_slow_non_contiguous=True)

    def bc_mid(ap, k):
        p, f = ap.shape
        return ap.unsqueeze(1).to_broadcast([p, k, f])

    def bc_in(ap, f):
        p, k = ap.shape
        return ap.unsqueeze(2).to_broadcast([p, k, f])

    def tt(eng, out, in0, in1, op, reads, writes):
        P.op(eng, lambda e: e.tensor_tensor(out=out, in0=in0, in1=in1, op=op), reads=reads, writes=writes)

    def act(out, in_, func, reads, writes, **kw):
        P.op("act", lambda e: e.activation(out=out, in_=in_, func=func, **kw), reads=reads, writes=writes)

    def mm(out, o_r, lhsT, l_r, rhs, r_r, start=True, stop=True):
        P.op("pe", lambda e: e.matmul(out, lhsT, rhs, start=start, stop=stop), reads=[l_r, r_r], writes=[o_r])

    def tr(out, o_r, in_, i_r, ident):
        P.op("pe", lambda e: e.transpose(out, in_, ident), reads=[i_r, cm], writes=[o_r])

    inv_t = {}

    def inv_tiles(key):
        if key not in inv_t:
            b0 = {"inv0": 12, "inv1": 18}[key]
            inv_t[key] = dict(A=[Qv(b0), Qv(b0 + 1)], AT=[Qv(b0 + 2), Qv(b0 + 3)], Pm=[Qv(b0 + 4), Qv(b0 + 5)])
        return inv_t[key]

    def inverse(key, X, XT, nch):
        tl = inv_tiles(key)
        Pm = tl['Pm'][0]
        tt("pool", Pm[:, :nch, :], bc_mid(I64, nch), X[:, :nch, :], ALU.subtract, [cm, X], [Pm])
        A, AT = X, XT
        for lvl in range(5):
            A2, A2T = tl['A'][lvl % 2], tl['AT'][lvl % 2]
            b2 = psum()
            for c in range(nch):
                mm(b2[0:64, c * 64:(c + 1) * 64], b2, A[:, c, :], A, AT[:, c, :], AT)
            if lvl < 4:
                b1 = psum()
                for c in range(nch):
                    mm(b1[0:64, c * 64:(c + 1) * 64], b1, AT[:, c, :], AT, A[:, c, :], A)
            P.op("dve", lambda e, A2T=A2T, b2=b2: e.tensor_copy(out=A2T[:, :nch, :], in_=b2[0:64, 0:nch * 64].rearrange("p (c f) -> p c f", f=64)),
                 reads=[b2], writes=[A2T])
            if lvl < 4:
                act(A2[:, :nch, :], b1[0:64, 0:nch * 64].rearrange("p (c f) -> p c f", f=64), AF.Copy, [b1], [A2])
            yield
            b3 = psum()
            for c in range(nch):
                mm(b3[0:64, c * 64:(c + 1) * 64], b3, A2T[:, c, :], A2T, Pm[:, c, :], Pm)
            Pn = tl['Pm'][(lvl + 1) % 2]
            tt("dve", Pn[:, :nch, :], b3[0:64, 0:nch * 64].rearrange("p (c f) -> p c f", f=64), Pm[:, :nch, :], ALU.add, [b3, Pm], [Pn])
            A, AT, Pm = A2, A2T, Pn
            yield
        return Pm

    MP_GC = 0
    MP_NW = 24
    MP_AL = 25
    MP_DT = 27
    if 'gdn' in do:
        g = {}
        scs = [P.sb([4, 1024], F32, f"g_sc{i}") for i in range(2)]
        scT = P.sb([64, nchtot, 4], F32, "g_scT")
        for pi, (c0, k, scsrc) in enumerate(d['sc_pieces']):
            sc = scs[pi % 2]
            b = psum()
            P.dma("sp", (sc[:, 0:k * 64], sc), scsrc)
            for c in range(k):
                tr(b[0:64, c * 4:(c + 1) * 4], b, sc[0:4, c * 64:(c + 1) * 64], sc, cm[0:4, 0, 0:4])
            act(scT[:, c0:c0 + k, :], b[0:64, 0:k * 4].rearrange("p (c f) -> p c f", f=4), AF.Copy, [b], [scT])
        g['beta'] = []; g['lg'] = []; g['neG'] = []; g['eGL'] = []; g['egl'] = []
        nea = P.sb([64, 2], F32, "g_nea")
        act(nea[:], mp[0:64, MP_AL:MP_AL + 2], AF.Exp, [mp], [nea])
        P.op("dve", lambda e: e.tensor_scalar(out=nea[:], in0=nea[:], scalar1=-1.0, scalar2=None, op0=ALU.mult), reads=[nea], writes=[nea])
        for j in range(2):
            beta = P.sb([64, nchtot], F32, f"g_beta{j}")
            lg = P.sb([64, nchtot], F32, f"g_lg{j}")
            neG = P.sb([64, nchtot], F32, f"g_neG{j}")
            eGL = P.sb([64, nchtot], F32, f"g_eGL{j}")
            egl = P.sb([128, nchtot], F32, f"g_egl{j}")
            act(beta[:], scT[:, :, j], AF.Sigmoid, [scT], [beta])
            act(lg[:], scT[:, :, 2 + j], AF.Exp, [scT, mp], [lg], bias=mp[0:64, MP_DT + j:MP_DT + j + 1])
            P.op("dve", lambda e, lg=lg: e.tensor_scalar(out=lg[:], in0=lg[:], scalar1=1.0, scalar2=None, op0=ALU.add), reads=[lg], writes=[lg])
            act(lg[:], lg[:], AF.Ln, [lg], [lg])
            P.op("dve", lambda e, lg=lg, j=j: e.tensor_scalar(out=lg[:], in0=lg[:], scalar1=nea[:, j:j + 1], scalar2=None, op0=ALU.mult), reads=[lg, nea], writes=[lg])
            b = psum()
            mm(b[0:64, 0:nchtot], b, UT, cm, lg[:], lg)
            act(neG[:], b[0:64, 0:nchtot], AF.Exp, [b], [neG])
            P.op("dve", lambda e, neG=neG: e.tensor_scalar(out=neG[:], in0=neG[:], scalar1=-1.0, scalar2=None, op0=ALU.mult), reads=[neG], writes=[neG])
            b = psum()
            mm(b[0:64, 0:nchtot], b, SLm, cm, lg[:], lg)
            act(eGL[:], b[0:64, 0:nchtot], AF.Exp, [b], [eGL])
            b = psum()
            mm(b[:, 0:nchtot], b, ONES[0:64, :], cm, lg[:], lg)
            act(egl[:], b[:, 0:nchtot], AF.Exp, [b], [egl])
            g['beta'].append(beta); g['lg'].append(lg); g['neG'].append(neG); g['eGL'].append(eGL); g['egl'].append(egl)
        gs = []
        for j in range(2):
            G = {}
            fb = 16 * j
            G['raw'] = Fv(fb + 0, 4, shape=(3, 515))
            G['cs'] = Fv(fb + 4, 3); G['z'] = Fv(fb + 7); G['sq'] = Fv(fb + 8, 2); G['kqn'] = Fv(fb + 10, 2)
            G['wpnT'] = Fv(fb + 12); G['qdec'] = Fv(fb + 13); G['oT'] = Fv(fb + 14); G['o2'] = Fv(fb + 15)
            qsl = [0, 1, 2, 3, 4, 5, 6] if j == 0 else [7, 8, 9, 10, 11, 24, 25]
            for qi, nm in zip(qsl, ['LGS', 'dec', 'decS', 'decI', 'X', 'XT', 'qkT']):
                G[nm] = Qv(qi)
            for ri, nm in enumerate(['kgn', 'kd', 'vtm', 'lgB']):
                G[nm] = Rv(4 * j + ri)
            G['vn'] = P.sb([64, 128], F32, f"g_vn{j}")
            gs.append(G)
        g['S'] = [[P.sb([128, 128], F32, f"g_S{j}_{i}") for i in range(2)] for j in range(2)]
        g['si'] = [0, 0]
        for j in range(2):
            P.op("pool", lambda e, j=j: e.memset(g['S'][j][0][:], 0.0), writes=[g['S'][j][0]])

    def gdn_gen(j, c0, nch, first):
        n = nch * 64
        cg0 = c0 // 64
        G = gs[j]
        raw, z, cs, sq, kqn = G['raw'], G['z'], G['cs'], G['sq'], G['kqn']
        beta, lg, neG, eGL, egl = g['beta'][j], g['lg'][j], g['neG'][j], g['eGL'][j], g['egl'][j]
        for ty in range(3):
            ld(raw[:, ty, 0:3 + n], raw, j * 512 + ty * 128, 128, c0, n, halo=3)
        ld(z[:, :n], z, j * 512 + 384, 128, c0, n)
        for ty in range(3):
            cwl = [mp[:, MP_GC + j * 12 + ty * 4 + tap: MP_GC + j * 12 + ty * 4 + tap + 1] for tap in range(4)]
            cw = lambda tap, cwl=cwl: cwl[tap]
            P.op("dve", lambda e, ty=ty, cw=cw: e.tensor_scalar(out=cs[:, ty, :n], in0=raw[:, ty, 0:n], scalar1=cw(0), scalar2=None, op0=ALU.mult),
                 reads=[raw, mp], writes=[cs])
            for tap in range(1, 4):
                P.op("dve", lambda e, ty=ty, cw=cw, tap=tap: e.scalar_tensor_tensor(out=cs[:, ty, :n], in0=raw[:, ty, tap:tap + n], scalar=cw(tap), op0=ALU.mult,
                                                                                 in1=cs[:, ty, :n], op1=ALU.add), reads=[raw, mp, cs], writes=[cs])
        act(cs[:, :, :n], cs[:, :, :n], AF.Silu, [cs], [cs])
        yield
        tt("pool", sq[:, :, :n], cs[:, 0:2, :n], cs[:, 0:2, :n], ALU.mult, [cs], [sq])
        for a in range(2):
            b = psum()
            mm(b[:, :n], b, ONES, cm, sq[:, a, :n], sq)
            P.op("dve", lambda e, b=b, a=a: e.tensor_scalar(out=sq[:, a, :n], in0=b[:, :n], scalar1=1e-6, scalar2=None, op0=ALU.add), reads=[b], writes=[sq])
        act(sq[:, :, :n], sq[:, :, :n], AF.Ln, [sq], [sq])
        act(sq[:, :, :n], sq[:, :, :n], AF.Exp, [sq], [sq], scale=-0.5)
        tt("dve", kqn[:, 0, :n], cs[:, 1, :n], sq[:, 1, :n], ALU.mult, [cs, sq], [kqn])
        P.op("dve", lambda e: e.scalar_tensor_tensor(out=kqn[:, 1, :n], in0=cs[:, 0, :n], scalar=128.0 ** -0.5, op0=ALU.mult, in1=sq[:, 0, :n], op1=ALU.mult),
             reads=[cs, sq], writes=[kqn])
        yield
        LGS, dec, decS, decI, X, XT, qkT = G['LGS'], G['dec'], G['decS'], G['decI'], G['X'], G['XT'], G['qkT']
        kgn, kd, vtm, wpnT, lgB, qdec, oT = G['kgn'], G['kd'], G['vtm'], G['wpnT'], G['lgB'], G['qdec'], G['oT']
        lgs = lg[:, cg0:cg0 + nch]
        tt("pool", LGS[:, :nch, :], bc_mid(SLm, nch), bc_in(lgs, 64), ALU.mult, [cm, lg], [LGS])
        b = psum()
        for c in range(nch):
            mm(b[0:64, c * 64:(c + 1) * 64], b, LGS[:, c, :], LGS, UT, cm)
        act(dec[:, :nch, :], b[0:64, 0:n].rearrange("p (c f) -> p c f", f=64), AF.Exp, [b], [dec])
        tt("pool", decS[:, :nch, :], dec[:, :nch, :], bc_mid(SU, nch), ALU.mult, [dec, cm], [decS])
        tt("pool", decS[:, :nch, :], decS[:, :nch, :], bc_in(beta[:, cg0:cg0 + nch], 64), ALU.mult, [decS, beta], [decS])
        tt("pool", decI[:, :nch, :], dec[:, :nch, :], bc_mid(UT, nch), ALU.mult, [dec, cm], [decI])
        yield
        bkk = psum(); bkq = psum()
        for c in range(nch):
            ks = kqn[:, 0, c * 64:(c + 1) * 64]
            mm(bkk[0:64, c * 64:(c + 1) * 64], bkk, ks, kqn, ks, kqn)
            mm(bkq[0:64, c * 64:(c + 1) * 64], bkq, ks, kqn, kqn[:, 1, c * 64:(c + 1) * 64], kqn)
        tt("dve", X[:, :nch, :], bkk[0:64, 0:n].rearrange("p (c f) -> p c f", f=64), decS[:, :nch, :], ALU.mult, [bkk, decS], [X])
        tt("dve", qkT[:, :nch, :], bkq[0:64, 0:n].rearrange("p (c f) -> p c f", f=64), decI[:, :nch, :], ALU.mult, [bkq, decI], [qkT])
        yield
        b = psum()
        for c in range(nch):
            tr(b[0:64, c * 64:(c + 1) * 64], b, X[:, c, :], X, I64)
        act(XT[:, :nch, :], b[0:64, 0:n].rearrange("p (c f) -> p c f", f=64), AF.Copy, [b], [XT])
        yield
        T2T = yield from inverse(f"inv{j}", X, XT, nch)
        yield
        for h0 in range(0, nch, 4):
            k4 = min(4, nch - h0)
            b = psum()
            for c in range(k4):
                tr(b[0:64, c * 128:(c + 1) * 128], b, kqn[:, 0, (h0 + c) * 64:(h0 + c + 1) * 64], kqn, I128)
            bv = b[0:64, 0:k4 * 128].rearrange("p (c f) -> p c f", f=128)
            tt("dve", kgn[:, h0:h0 + k4, :], bv, bc_in(neG[:, cg0 + h0:cg0 + h0 + k4], 128), ALU.mult, [b, neG], [kgn])
            tt("dve", kd[:, h0:h0 + k4, :], bv, bc_in(eGL[:, cg0 + h0:cg0 + h0 + k4], 128), ALU.mult, [b, eGL], [kd])
            b = psum()
            for c in range(k4):
                tr(b[0:64, c * 128:(c + 1) * 128], b, cs[:, 2, (h0 + c) * 64:(h0 + c + 1) * 64], cs, I128)
            act(vtm[:, h0:h0 + k4, :], b[0:64, 0:k4 * 128].rearrange("p (c f) -> p c f", f=128), AF.Copy, [b], [vtm])
            yield
        b = psum()
        for c in range(nch):
            mm(b[:, c * 64:(c + 1) * 64], b, kgn[:, c, :], kgn, T2T[:, c, :], T2T)
        act(wpnT[:, :n], b[:, :n], AF.Copy, [b], [wpnT])
        yield
        tt("pool", lgB[:, :nch, :], bc_mid(ONES[0:64, :], nch), bc_in(lgs, 128), ALU.mult, [cm, lg], [lgB])
        b = psum()
        for c in range(nch):
            mm(b[:, c * 64:(c + 1) * 64], b, lgB[:, c, :], lgB, UT, cm)
        act(qdec[:, :n], b[:, :n], AF.Exp, [b], [qdec])
        tt("dve", qdec[:, :n], qdec[:, :n], kqn[:, 1, :n], ALU.mult, [qdec, kqn], [qdec])

        def step(c):
            cg = cg0 + c
            S = g['S'][j][g['si'][j] % 2]
            Sn = g['S'][j][(g['si'][j] + 1) % 2]
            g['si'][j] += 1
            vn = G['vn']
            pv = psum()
            mm(pv[0:64, 0:128], pv, T2T[:, c, :], T2T, vtm[:, c, :], vtm, start=True, stop=False)
            mm(pv[0:64, 0:128], pv, wpnT[:, c * 64:(c + 1) * 64], wpnT, S[:], S, start=False, stop=True)
            act(vn[:], pv[0:64, 0:128], AF.Identity, [pv, beta], [vn], scale=beta[:, cg:cg + 1])
            yield
            po = psum()
            mm(po[:, 0:64], po, S[:], S, qdec[:, c * 64:(c + 1) * 64], qdec, start=True, stop=False)
            mm(po[:, 0:64], po, vn[:], vn, qkT[:, c, :], qkT, start=False, stop=True)
            act(oT[:, c * 64:(c + 1) * 64], po[:, 0:64], AF.Copy, [po], [oT])
            pS = psum()
            mm(pS[:, 0:128], pS, kd[:, c, :], kd, vn[:], vn)
            P.op("dve", lambda e: e.scalar_tensor_tensor(out=Sn[:], in0=S[:], scalar=egl[:, cg:cg + 1], op0=ALU.mult, in1=pS[:, 0:128], op1=ALU.add),
                 reads=[S, egl, pS], writes=[Sn])
            yield

        def post():
            o2 = G['o2']
            tt("pool", o2[:, :n], oT[:, :n], oT[:, :n], ALU.mult, [oT], [o2])
            b = psum()
            mm(b[:, :n], b, ONES, cm, o2[:, :n], o2)
            P.op("dve", lambda e: e.tensor_scalar(out=o2[:, :n], in0=b[:, :n], scalar1=1.0 / 128, scalar2=LN_EPS, op0=ALU.mult, op1=ALU.add), reads=[b], writes=[o2])
            act(o2[:, :n], o2[:, :n], AF.Ln, [o2], [o2])
            act(o2[:, :n], o2[:, :n], AF.Exp, [o2], [o2], scale=-0.5)
            tt("dve", oT[:, :n], oT[:, :n], o2[:, :n], ALU.mult, [oT, o2], [oT])
            act(z[:, :n], z[:, :n], AF.Silu, [z], [z])
            P.op("dve", lambda e: e.scalar_tensor_tensor(out=oT[:, :n], in0=oT[:, :n], scalar=mp[:, MP_NW:MP_NW + 1], op0=ALU.mult, in1=z[:, :n], op1=ALU.mult),
                 reads=[oT, mp, z], writes=[oT])
            for dst in d['ydst'](yrows['g'][j], c0, n):
                P.dma("pool", dst, (oT[:, :n], oT), slot=f"o_g_oT{j}")
        yield
        for c in range(nch):
            yield from step(c)
        post()


    MU0 = 29; W0 = 33; A0 = 34; KKc = 35; KAc = 36; LNG = 37; LNB = 38; RKc = 39; RNG = 40; RNB = 41; HM0 = 42; GAM = 44
    col = lambda cidx: mp[:, cidx:cidx + 1]

    def ts(eng, out, in0, s1, op0, reads, writes, s2=None, op1=None):
        if op1 is None:
            P.op(eng, lambda e: e.tensor_scalar(out=out, in0=in0, scalar1=s1, scalar2=None, op0=op0), reads=reads, writes=writes)
        else:
            P.op(eng, lambda e: e.tensor_scalar(out=out, in0=in0, scalar1=s1, scalar2=s2, op0=op0, op1=op1), reads=reads, writes=writes)

    def stt(eng, out, in0, sc_, op0, in1, op1, reads, writes):
        P.op(eng, lambda e: e.scalar_tensor_tensor(out=out, in0=in0, scalar=sc_, op0=op0, in1=in1, op1=op1), reads=reads, writes=writes)

    def v3(b, nch_, f, p=64):
        return b[0:p, 0:nch_ * f].rearrange("p (c f) -> p c f", f=f)

    if 'rwkv' in do:
        w = {}
        w['raw'] = Fv(0, 5, shape=(4, 513))
        w['pf'] = Fv(5, 4)
        for fi, nm in enumerate(['lrt', 'sgw', 'a', 'gT', 'kk', 'kp', 'bs', 't1', 't2', 'Gd', 'eA', 'rt', 'atp', 'btm0', 'btm1', 'ktm0', 'ktm1', 'bh', 'kh', 'WT0', 'WT1', 'yT']):
            w[nm] = Fv(9 + fi)
        for qi, nm in enumerate(['X0', 'X1', 'XT0', 'XT1', 'Aak0', 'Aak1', 'Arb0', 'Arb1', 'Ark0', 'Ark1', 'M10', 'M11']):
            w[nm] = Qv(qi)
        for ri, nm in enumerate(['Vp0', 'Vp1', 'Atm0', 'Atm1', 'Bh0', 'Bh1', 'Kh0', 'Kh1']):
            w[nm] = Rv(ri)
        w['U'] = P.sb([64, 2, 128], F32, "w_U")
        w['GC'] = P.sb([128, 8], F32, "w_GC")
        w['omk'] = P.sb([128, 1], F32, "w_omk")
        w['rmask'] = P.sb([128, 512], F32, "w_rmask")
        w['S'] = [P.sb([128, 128], F32, f"w_S{i}") for i in range(2)]
        w['si'] = 0
        P.op("pool", lambda e: e.memset(w['S'][0][:], 0.0), writes=[w['S'][0]])
        P.op("pool", lambda e: e.memset(w['rmask'][:], 1.0), writes=[w['rmask']])
        P.op("pool", lambda e: e.memset(w['rmask'][:].rearrange("p (c f) -> p c f", f=64)[:, :, 0:1], 0.0), writes=[w['rmask']])
        ts("dve", w['omk'][:], col(KAc), -1.0, ALU.mult, [mp], [w['omk']], s2=1.0, op1=ALU.add)

    def rwkv_tile(c0, nch, first):
        n = nch * 64
        raw, pf = w['raw'], w['pf']
        for ty in range(4):
            ld(raw[:, ty, 0:1 + n], raw, 1024 + ty * 128, 128, c0, n, halo=1)
        tt("dve", pf[:, :, :n], raw[:, :, 0:n], raw[:, :, 1:1 + n], ALU.subtract, [raw], [pf])
        for ty in range(4):
            stt("dve", pf[:, ty, :n], pf[:, ty, :n], col(MU0 + ty), ALU.mult, raw[:, ty, 1:1 + n], ALU.add, [pf, mp, raw], [pf])
        r_, k_, v_ = pf[:, 0, :n], pf[:, 1, :n], pf[:, 2, :n]
        lrt, sgw, a_, gT, kk, kp, bs, t1, t2, Gd, eA = [w[x] for x in ['lrt', 'sgw', 'a', 'gT', 'kk', 'kp', 'bs', 't1', 't2', 'Gd', 'eA']]
        act(lrt[0:32, :n], pf[0:32, 3, :n], AF.Tanh, [pf], [lrt])
        act(lrt[32:64, :n], pf[32:64, 3, :n], AF.Copy, [pf], [lrt])
        act(lrt[64:128, :n], pf[64:128, 3, :n], AF.Sigmoid, [pf], [lrt])
        b = psum(); mm(b[:, :n], b, LOWUP[0:32, :], cm, lrt[0:32, :n], lrt)
        act(sgw[:, :n], b[:, :n], AF.Sigmoid, [b, mp], [sgw], bias=col(W0))
        b = psum(); mm(b[:, :n], b, LOWUP[32:64, :], cm, lrt[32:64, :n], lrt)
        act(a_[:, :n], b[:, :n], AF.Sigmoid, [b, mp], [a_], bias=col(A0))
        b = psum(); mm(b[:, :n], b, LOWUP[64:128, :], cm, lrt[64:128, :n], lrt)
        act(gT[:, :n], b[:, :n], AF.Copy, [b], [gT])
        ts("dve", kk[:, :n], k_, col(KKc), ALU.mult, [pf, mp], [kk])
        tt("pool", t1[:, :n], kk[:, :n], kk[:, :n], ALU.mult, [kk], [t1])
        b = psum(); mm(b[:, :n], b, BLK, cm, t1[:, :n], t1)
        ts("dve", t1[:, :n], b[:, :n], 1e-6, ALU.add, [b], [t1])
        act(t1[:, :n], t1[:, :n], AF.Ln, [t1], [t1])
        act(t1[:, :n], t1[:, :n], AF.Exp, [t1], [t1], scale=-0.5)
        tt("dve", kk[:, :n], kk[:, :n], t1[:, :n], ALU.mult, [kk, t1], [kk])
        ts("dve", t2[:, :n], a_[:, :n], col(KAc), ALU.mult, [a_, mp, w['omk']], [t2], s2=w['omk'][:, 0:1], op1=ALU.add)
        tt("dve", kp[:, :n], k_, t2[:, :n], ALU.mult, [pf, t2], [kp])
        tt("pool", bs[:, :n], kk[:, :n], a_[:, :n], ALU.mult, [kk, a_], [bs])
        ts("dve", sgw[:, :n], sgw[:, :n], -0.6065306597126334, ALU.mult, [sgw], [sgw])
        P.op("dve", lambda e: e.tensor_tensor_scan(out=Gd[:, :n], data0=w['rmask'][:, :n], data1=sgw[:, :n], initial=0.0, op0=ALU.mult, op1=ALU.add),
             reads=[w['rmask'], sgw], writes=[Gd])
        rt_, atp, bh, kh = w['rt'], w['atp'], w['bh'], w['kh']
        act(eA[:, :n], Gd[:, :n], AF.Exp, [Gd], [eA])
        tt("dve", rt_[:, :n], r_, eA[:, :n], ALU.mult, [pf, eA], [rt_])
        tt("pool", t1[:, :n], Gd[:, :n], sgw[:, :n], ALU.subtract, [Gd, sgw], [t1])
        act(eA[:, :n], t1[:, :n], AF.Exp, [t1], [eA])
        tt("dve", atp[:, :n], kk[:, :n], eA[:, :n], ALU.mult, [kk, eA], [atp])
        act(eA[:, :n], Gd[:, :n], AF.Exp, [Gd], [eA], scale=-1.0)
        for j in range(2):
            stt("dve", w[f'btm{j}'][:, :n], bs[:, :n], col(HM0 + j), ALU.mult, eA[:, :n], ALU.mult, [bs, mp, eA], [w[f'btm{j}']])
            stt("dve", w[f'ktm{j}'][:, :n], kp[:, :n], col(HM0 + j), ALU.mult, eA[:, :n], ALU.mult, [kp, mp, eA], [w[f'ktm{j}']])
        Gd3 = Gd[:, :n].rearrange("p (c f) -> p c f", f=64)
        last = Gd3[:, :, 63]
        tt("pool", t1[:, :n].rearrange("p (c f) -> p c f", f=64), bc_in(last, 64), Gd3, ALU.subtract, [Gd], [t1])
        act(eA[:, :n], t1[:, :n], AF.Exp, [t1], [eA])
        tt("dve", bh[:, :n], bs[:, :n], eA[:, :n], ALU.mult, [bs, eA], [bh])
        tt("dve", kh[:, :n], kp[:, :n], eA[:, :n], ALU.mult, [kp, eA], [kh])
        GC = w['GC']
        act(GC[:, :nch], last, AF.Exp, [Gd], [GC])
        for h0 in range(0, nch, 4):
            k4 = min(4, nch - h0)
            def trb(src, sr):
                b = psum()
                for c in range(k4):
                    tr(b[0:64, c * 128:(c + 1) * 128], b, src[:, (h0 + c) * 64:(h0 + c + 1) * 64], sr, I128)
                return b
            b = trb(v_, pf)
            for j in range(2):
                tt("dve", w[f'Vp{j}'][:, h0:h0 + k4, :], v3(b, k4, 128), bc_mid(HMc[j], k4), ALU.mult, [b, cm], [w[f'Vp{j}']])
            b = trb(atp, atp)
            for j in range(2):
                stt("dve", w[f'Atm{j}'][:, h0:h0 + k4, :], v3(b, k4, 128), -1.0, ALU.mult, bc_mid(HMc[j], k4), ALU.mult, [b, cm], [w[f'Atm{j}']])
            b = trb(bh, bh)
            for j in range(2):
                tt("dve", w[f'Bh{j}'][:, h0:h0 + k4, :], v3(b, k4, 128), bc_mid(HMc[j], k4), ALU.mult, [b, cm], [w[f'Bh{j}']])
            b = trb(kh, kh)
            for j in range(2):
                tt("dve", w[f'Kh{j}'][:, h0:h0 + k4, :], v3(b, k4, 128), bc_mid(HMc[j], k4), ALU.mult, [b, cm], [w[f'Kh{j}']])
        TT = [None, None]

        def head_gen(j):
            btm, ktm = w[f'btm{j}'], w[f'ktm{j}']
            X, XT, Aak, Arb, Ark = w[f'X{j}'], w[f'XT{j}'], w[f'Aak{j}'], w[f'Arb{j}'], w[f'Ark{j}']
            def grp(lh, lr_, rh, rr):
                b = psum()
                for c in range(nch):
                    mm(b[0:64, c * 64:(c + 1) * 64], b, lh[:, c * 64:(c + 1) * 64], lr_, rh[:, c * 64:(c + 1) * 64], rr)
                return b
            b = grp(btm, btm, atp, atp)
            tt("dve", X[:, :nch, :], v3(b, nch, 64), bc_mid(SU, nch), ALU.mult, [b, cm], [X])
            yield
            b = grp(atp, atp, btm, btm)
            tt("dve", XT[:, :nch, :], v3(b, nch, 64), bc_mid(SLm, nch), ALU.mult, [b, cm], [XT])
            yield
            b = grp(atp, atp, ktm, ktm)
            stt("dve", Aak[:, :nch, :], v3(b, nch, 64), -1.0, ALU.mult, bc_mid(SLm, nch), ALU.mult, [b, cm], [Aak])
            yield
            b = grp(btm, btm, rt_, rt_)
            tt("dve", Arb[:, :nch, :], v3(b, nch, 64), bc_mid(UT, nch), ALU.mult, [b, cm], [Arb])
            yield
            b = grp(ktm, ktm, rt_, rt_)
            tt("dve", Ark[:, :nch, :], v3(b, nch, 64), bc_mid(UT, nch), ALU.mult, [b, cm], [Ark])
            yield
            TTj = yield from inverse(f"inv{j}", X, XT, nch)
            TT[j] = TTj
            Atm, WT, Aak, M1 = w[f'Atm{j}'], w[f'WT{j}'], w[f'Aak{j}'], w[f'M1{j}']
            b = psum()
            for c in range(nch):
                mm(b[:, c * 64:(c + 1) * 64], b, Atm[:, c, :], Atm, TT[j][:, c, :], TT[j])
            act(WT[:, :n], b[:, :n], AF.Copy, [b], [WT])
            yield
            b = psum()
            for c in range(nch):
                mm(b[0:64, c * 64:(c + 1) * 64], b, Aak[:, c, :], Aak, TT[j][:, c, :], TT[j])
            act(M1[:, :nch, :], v3(b, nch, 64), AF.Copy, [b], [M1])
        run_rr([head_gen(0), head_gen(1)])
        yT = w['yT']

        def step(c):
            S = w['S'][w['si'] % 2]; Sn = w['S'][(w['si'] + 1) % 2]; w['si'] += 1
            U = w['U']
            cs_ = slice(c * 64, (c + 1) * 64)
            pu = psum()
            for j in range(2):
                mm(pu[0:64, j * 128:(j + 1) * 128], pu, w[f'WT{j}'][:, cs_], w[f'WT{j}'], S[:], S, start=True, stop=False)
                mm(pu[0:64, j * 128:(j + 1) * 128], pu, w[f'M1{j}'][:, c, :], w[f'M1{j}'], w[f'Vp{j}'][:, c, :], w[f'Vp{j}'], start=False, stop=True)
            act(U[:], pu[0:64, 0:256].rearrange("p (j f) -> p j f", f=128), AF.Copy, [pu], [U])
            py = psum()
            mm(py[:, 0:64], py, S[:], S, rt_[:, cs_], rt_, start=True, stop=False)
            for j in range(2):
                mm(py[:, 0:64], py, U[:, j, :], U, w[f'Arb{j}'][:, c, :], w[f'Arb{j}'], start=False, stop=False)
                mm(py[:, 0:64], py, w[f'Vp{j}'][:, c, :], w[f'Vp{j}'], w[f'Ark{j}'][:, c, :], w[f'Ark{j}'], start=False, stop=(j == 1))
            act(yT[:, cs_], py[:, 0:64], AF.Copy, [py], [yT])
            pS = psum()
            for j in range(2):
                mm(pS[:, 0:128], pS, w[f'Bh{j}'][:, c, :], w[f'Bh{j}'], U[:, j, :], U, start=(j == 0), stop=False)
                mm(pS[:, 0:128], pS, w[f'Kh{j}'][:, c, :], w[f'Kh{j}'], w[f'Vp{j}'][:, c, :], w[f'Vp{j}'], start=False, stop=(j == 1))
            stt("dve", Sn[:], S[:], GC[:, c:c + 1], ALU.mult, pS[:, 0:128], ALU.add, [S, GC, pS], [Sn])

        def post():
            b = psum(); mm(b[:, :n], b, BLK, cm, yT[:, :n], yT)
            stt("dve", yT[:, :n], b[:, :n], -1.0 / 64, ALU.mult, yT[:, :n], ALU.add, [b, yT], [yT])
            tt("pool", t1[:, :n], yT[:, :n], yT[:, :n], ALU.mult, [yT], [t1])
            b = psum(); mm(b[:, :n], b, BLK, cm, t1[:, :n], t1)
            ts("dve", t1[:, :n], b[:, :n], 1.0 / 64, ALU.mult, [b], [t1], s2=64e-5, op1=ALU.add)
            act(t1[:, :n], t1[:, :n], AF.Ln, [t1], [t1])
            act(t1[:, :n], t1[:, :n], AF.Exp, [t1], [t1], scale=-0.5)
            tt("dve", yT[:, :n], yT[:, :n], t1[:, :n], ALU.mult, [yT, t1], [yT])
            ts("dve", yT[:, :n], yT[:, :n], col(LNG), ALU.mult, [yT, mp], [yT], s2=col(LNB), op1=ALU.add)
            tt("pool", t2[:, :n], r_, kp[:, :n], ALU.mult, [pf, kp], [t2])
            ts("dve", t2[:, :n], t2[:, :n], col(RKc), ALU.mult, [t2, mp], [t2])
            b = psum(); mm(b[:, :n], b, BLK, cm, t2[:, :n], t2)
            tt("dve", t2[:, :n], b[:, :n], v_, ALU.mult, [b, pf], [t2])
            tt("pool", yT[:, :n], yT[:, :n], t2[:, :n], ALU.add, [yT, t2], [yT])
            tt("dve", yT[:, :n], yT[:, :n], gT[:, :n], ALU.mult, [yT, gT], [yT])
            for dst in d['ydst'](yrows['w'], c0, n):
                P.dma("pool", dst, (yT[:, :n], yT), slot="o_w_yT")
        return step, post

    if 'ret' in do:
        rr = {}
        for fi, nm in enumerate(['QA', 'KA', 'QB', 'KB', 'COS', 'SIN', 'qr', 'kr', 'qd', 'kdc', 't']):
            rr[nm] = Fv(fi, parts=64)
        for fi, nm in enumerate(['v', 'gate', 'oT', 't1', 'tA', 'tB']):
            rr[nm] = Fv(11 + fi)
        for qi, nm in enumerate(['qk0', 'qk1', 'Kd0', 'Kd1']):
            rr[nm] = Qv(qi)
        for ri, nm in enumerate(['Vp0', 'Vp1']):
            rr[nm] = Rv(ri)
        rr['S'] = [P.sb([64, 128], F32, f"r_S{i}") for i in range(2)]
        rr['si'] = 0
        P.op("pool", lambda e: e.memset(rr['S'][0][:], 0.0), writes=[rr['S'][0]])

    def ret_tile(c0, nch):
        n = nch * 64
        QA, KA, QB, KB, COS, SIN, qr, kr, qd, kdc, t_ = [rr[x] for x in ['QA', 'KA', 'QB', 'KB', 'COS', 'SIN', 'qr', 'kr', 'qd', 'kdc', 't']]
        v_, gate, oT, t1 = rr['v'], rr['gate'], rr['oT'], rr['t1']
        base = 12 * 128
        for tl_, r0 in [(QA, base), (KA, base + 64), (QB, base + 128), (KB, base + 192)]:
            ld(tl_[:, :n], tl_, r0, 64, c0, n)
        ld(v_[:, :n], v_, base + 256, 128, c0, n)
        ld(gate[:, :n], gate, base + 384, 128, c0, n)
        P.dma("sp", (COS[:, :n], COS), (d['rt'][0, :, c0:c0 + n], d['rt']))
        P.dma("sp", (SIN[:, :n], SIN), (d['rt'][1, :, c0:c0 + n], d['rt']))
        tt("dve", qr[:, :n], QA[:, :n], COS[:, :n], ALU.mult, [QA, COS], [qr])
        tt("pool", t_[:, :n], QB[:, :n], SIN[:, :n], ALU.mult, [QB, SIN], [t_])
        tt("dve", qr[:, :n], qr[:, :n], t_[:, :n], ALU.add, [qr, t_], [qr])
        tt("dve", kr[:, :n], KA[:, :n], COS[:, :n], ALU.mult, [KA, COS], [kr])
        tt("pool", t_[:, :n], KB[:, :n], SIN[:, :n], ALU.mult, [KB, SIN], [t_])
        tt("dve", kr[:, :n], kr[:, :n], t_[:, :n], ALU.add, [kr, t_], [kr])
        q3 = lambda x: x[:, :n].rearrange("p (c f) -> p c f", f=64)
        tt("pool", q3(qd), q3(qr), bc_mid(QDT, nch), ALU.mult, [qr, cm], [qd])
        tt("pool", q3(kdc), q3(kr), bc_mid(KDT, nch), ALU.mult, [kr, cm], [kdc])
        for j in range(2):
            b = psum()
            for c in range(nch):
                mm(b[0:64, c * 64:(c + 1) * 64], b, kr[32 * j:32 * j + 32, c * 64:(c + 1) * 64], kr, qr[32 * j:32 * j + 32, c * 64:(c + 1) * 64], qr)
            tt("dve", rr[f'qk{j}'][:, :nch, :], v3(b, nch, 64), bc_mid(DTj[j], nch), ALU.mult, [b, cm], [rr[f'qk{j}']])
        for h0 in range(0, nch, 4):
            k4 = min(4, nch - h0)
            b = psum()
            for c in range(k4):
                tr(b[0:64, c * 128:(c + 1) * 128], b, v_[:, (h0 + c) * 64:(h0 + c + 1) * 64], v_, I128)
            for j in range(2):
                tt("dve", rr[f'Vp{j}'][:, h0:h0 + k4, :], v3(b, k4, 128), bc_mid(HMc[j], k4), ALU.mult, [b, cm], [rr[f'Vp{j}']])
        b = psum()
        for c in range(nch):
            tr(b[0:64, c * 64:(c + 1) * 64], b, kdc[:, c * 64:(c + 1) * 64], kdc, I64)
        for j in range(2):
            tt("dve", rr[f'Kd{j}'][:, :nch, :], v3(b, nch, 64), bc_mid(HMr[j], nch), ALU.mult, [b, cm], [rr[f'Kd{j}']])

        def step(c):
            S = rr['S'][rr['si'] % 2]; Sn = rr['S'][(rr['si'] + 1) % 2]; rr['si'] += 1
            cs_ = slice(c * 64, (c + 1) * 64)
            po = psum()
            mm(po[:, 0:64], po, S[:], S, qd[:, cs_], qd, start=True, stop=False)
            for j in range(2):
                mm(po[:, 0:64], po, rr[f'Vp{j}'][:, c, :], rr[f'Vp{j}'], rr[f'qk{j}'][:, c, :], rr[f'qk{j}'], start=False, stop=(j == 1))
            act(oT[:, cs_], po[:, 0:64], AF.Copy, [po], [oT])
            pS = psum()
            for j in range(2):
                mm(pS[0:64, 0:128], pS, rr[f'Kd{j}'][:, c, :], rr[f'Kd{j}'], rr[f'Vp{j}'][:, c, :], rr[f'Vp{j}'], start=(j == 0), stop=(j == 1))
            stt("dve", Sn[:], S[:], mp[0:64, GAM:GAM + 1], ALU.mult, pS[0:64, 0:128], ALU.add, [S, mp, pS], [Sn])

        def post():
            b = psum(); mm(b[:, :n], b, BLK, cm, oT[:, :n], oT)
            stt("dve", oT[:, :n], b[:, :n], -1.0 / 64, ALU.mult, oT[:, :n], ALU.add, [b, oT], [oT])
            tt("pool", t1[:, :n], oT[:, :n], oT[:, :n], ALU.mult, [oT], [t1])
            b = psum(); mm(b[:, :n], b, BLK, cm, t1[:, :n], t1)
            ts("dve", t1[:, :n], b[:, :n], 1.0 / 64, ALU.mult, [b], [t1], s2=LN_EPS, op1=ALU.add)
            act(t1[:, :n], t1[:, :n], AF.Ln, [t1], [t1])
            act(t1[:, :n], t1[:, :n], AF.Exp, [t1], [t1], scale=-0.5)
            tt("dve", oT[:, :n], oT[:, :n], t1[:, :n], ALU.mult, [oT, t1], [oT])
            ts("dve", oT[:, :n], oT[:, :n], col(RNG), ALU.mult, [oT, mp], [oT], s2=col(RNB), op1=ALU.add)
            act(gate[:, :n], gate[:, :n], AF.Silu, [gate], [gate])
            tt("dve", oT[:, :n], oT[:, :n], gate[:, :n], ALU.mult, [oT, gate], [oT])
            for dst in d['ydst'](yrows['r'], c0, n):
                P.dma("pool", dst, (oT[:, :n], oT), slot="o_r_oT")
        return step, post

    for ti, (c0, nch) in enumerate(tiles):
        if 'gdn' in do:
            run_rr([gdn_gen(0, c0, nch, ti == 0), gdn_gen(1, c0, nch, ti == 0)])
        if 'rwkv' in do:
            step, post = rwkv_tile(c0, nch, ti == 0)
            for c in range(nch):
                step(c)
            post()
        if 'ret' in do:
            step, post = ret_tile(c0, nch)
            for c in range(nch):
                step(c)
            post()


import numpy as np
GDN_QKV = 1536; D_A = 512; D_A_IN = 2056; D_B_IN = 896; D_C_IN = 768


def proj_cols():
    cols = []
    for g in range(2):
        for j in range(2):
            h = 2 * g + j
            cols += list(range(h * 128, (h + 1) * 128))
            cols += list(range(512 + h * 128, 512 + (h + 1) * 128))
            cols += list(range(1024 + h * 128, 1024 + (h + 1) * 128))
            cols += list(range(GDN_QKV + h * 128, GDN_QKV + (h + 1) * 128))
        o = D_A_IN
        cols += list(range(o + g * 128, o + (g + 1) * 128))
        cols += list(range(o + 256 + g * 128, o + 256 + (g + 1) * 128))
        cols += list(range(o + 512 + g * 128, o + 512 + (g + 1) * 128))
        cols += list(range(o + 768, o + 896))
        o = D_A_IN + D_B_IN
        q = [o + (2 * g + j) * 32 + i for j in range(2) for i in range(32)]
        k = [o + 128 + (2 * g + j) * 32 + i for j in range(2) for i in range(32)]
        sw = lambda lst: [lst[j * 32 + (i + 16) % 32] for j in range(2) for i in range(32)]
        cols += q + k
        cols += sw(q) + sw(k)
        cols += list(range(o + 256 + g * 128, o + 256 + (g + 1) * 128))
        cols += list(range(o + 512 + g * 128, o + 512 + (g + 1) * 128))
    for g in range(2):
        o = GDN_QKV + D_A
        cols += [o + 2 * g, o + 2 * g + 1, o + 4 + 2 * g, o + 4 + 2 * g + 1]
    assert len(cols) == 4104
    return np.array(cols)


def lnp_pack(gs, bs):
    out = np.zeros((128, 48), np.float32)
    for i in range(3):
        out[:, i * 16: i * 16 + 8] = gs[i].reshape(8, 128).T
        out[:, i * 16 + 8: i * 16 + 16] = bs[i].reshape(8, 128).T
    return out


import numpy as np


def cm_pack(z, l, g):
    cm = np.zeros((128, 13, 128), np.float32)
    cm[:, 0, :] = np.eye(128)
    i = np.arange(64)
    cm[:64, 1, :64] = (i[:, None] <= i[None, :])
    cm[:64, 2, :64] = (i[:, None] > i[None, :])
    cm[:64, 3, :64] = (i[:, None] < i[None, :])
    cm[:, 4, :] = 1.0
    cm[:64, 5, :64] = 1.0; cm[64:, 5, 64:] = 1.0
    cm[0:32, 6, :] = z['rwkv_w_up'][l][:, g * 128:(g + 1) * 128]
    cm[32:64, 6, :] = z['rwkv_a_up'][l][:, g * 128:(g + 1) * 128]
    cm[64:128, 6, :] = z['rwkv_g_up'][l][:, g * 128:(g + 1) * 128]
    for j in range(2):
        h = 2 * g + j
        lgam = np.log(1.0 - 2.0 ** (-5.0 - h))
        diff = i[None, :] - i[:, None]
        cm[:64, 7, j * 64:(j + 1) * 64] = np.where(diff >= 0, np.exp(lgam * np.maximum(diff, 0)), 0.0) * 32 ** -0.5
    for j in range(2):
        h = 2 * g + j
        lgam = np.log(1.0 - 2.0 ** (-5.0 - h))
        cm[j * 32:(j + 1) * 32, 8, 0:64] = np.exp(lgam * (i + 1.0))[None, :]
        cm[j * 32:(j + 1) * 32, 8, 64:128] = np.exp(lgam * (63.0 - i))[None, :] * 32 ** -0.5
        cm[:64, 9 + j, j * 64:(j + 1) * 64] = 1.0
        cm[:64, 11 + j, j * 32:(j + 1) * 32] = 1.0
    return cm


def mp_pack(z, l, g):
    mp = np.zeros((128, 64), np.float32)
    for j in range(2):
        h = 2 * g + j
        for ty in range(3):
            for tap in range(4):
                mp[:, j * 12 + ty * 4 + tap] = z['gdn_conv_w'][l][tap, ty * 512 + h * 128: ty * 512 + (h + 1) * 128]
        mp[:, 25 + j] = z['gdn_a_log'][l][h]
        mp[:, 27 + j] = z['gdn_dt_bias'][l][h]
    mp[:, 24] = z['gdn_norm_w'][l]
    sl = slice(g * 128, (g + 1) * 128)
    mu = z['rwkv_mu'][l]
    mp[:, 29] = mu[0:256][sl]; mp[:, 30] = mu[256:512][sl]; mp[:, 31] = mu[512:768][sl]; mp[:, 32] = mu[768:896]
    mp[:, 33] = z['rwkv_w0'][l][sl]; mp[:, 34] = z['rwkv_a0'][l][sl]; mp[:, 35] = z['rwkv_k_k'][l][sl]; mp[:, 36] = z['rwkv_k_a'][l][sl]
    mp[:, 37] = z['rwkv_lnx_g'][l][sl]; mp[:, 38] = z['rwkv_lnx_b'][l][sl]; mp[:, 39] = z['rwkv_r_k'][l].reshape(-1)[sl]
    mp[:, 40] = z['ret_norm_g'][l][sl]; mp[:, 41] = z['ret_norm_b'][l][sl]
    mp[0:64, 42] = 1.0; mp[64:128, 43] = 1.0
    for j in range(2):
        mp[j * 32:(j + 1) * 32, 44] = (1.0 - 2.0 ** (-5.0 - (2 * g + j))) ** 64
    return mp


def rt_pack(ltot):
    pos = np.arange(ltot, dtype=np.float32)
    inv = (1.0 / (10000.0 ** np.linspace(0.0, 1.0, 16, dtype=np.float32))).astype(np.float32)
    ang = pos[None, :] * inv[:, None]
    cos, sin = np.cos(ang), np.sin(ang)
    rt = np.zeros((2, 64, ltot), np.float32)
    for j in range(2):
        rt[0, j * 32:j * 32 + 16] = cos; rt[0, j * 32 + 16:j * 32 + 32] = cos
        rt[1, j * 32:j * 32 + 16] = -sin; rt[1, j * 32 + 16:j * 32 + 32] = sin
    return rt


DEPTH = 2
NH = 4096
NL = 64 + NH
PAIRS = [[0, 1], [2, 3], [4, 5], [6, 7]]
I32 = mybir.dt.int32


def build_all():
    t_tiles = [(0, 64, True)] + [(64 + i * 512, 512, False) for i in range(NH // 512)]
    m_tiles = [(0, 1)] + [(64 + i * 512, 8) for i in range(2 * NH // 512)]
    nc = bass.Bass("TRN2", target_bir_lowering=False)
    P = Prog(nc)
    ext = lambda name, shape: P.dram(name, shape, F32, "ExternalInput")
    itn = lambda name, shape: P.dram(name, shape, F32, "Internal")
    sub = lambda t, ap: T(ap, t.res)
    ds = bass.ds
    gsel = P.dram("gsel", [1, 1], I32, "ExternalInput")
    P.dynsel = gsel.ap[0:1, 0:1]
    xT = ext("xT", [1024, NL])
    lnp = ext("lnp", [3, 128, 48])
    wf1i = ext("w_ff1_in", [DEPTH, 1024, 2 * D_FF]); wf1o = ext("w_ff1_out", [DEPTH, D_FF, 1024])
    wf2i = ext("w_ff2_in", [DEPTH, 1024, 2 * D_FF]); wf2o = ext("w_ff2_out", [DEPTH, D_FF, 1024])
    wp = ext("wp", [DEPTH, 1024, NPROJ]); wo = ext("w_out", [DEPTH, 1024, 1024])
    mp = ext("mp", [DEPTH, 128, 64]); cm = ext("cm", [DEPTH, 128, 13, 128]); rt = ext("rt", [2, 64, L])
    out = P.dram("out", [1024, NL], F32, "ExternalOutput")
    hA = itn("hA", [1024, NL]); hB = itn("hB", [1024, NL])
    pmL = itn("pmL", [4096, NH]); pm0L = itn("pm0L", [4096, 128]); pscL = itn("pscL", [8, NH]); psc0L = itn("psc0L", [8, 128])
    Gall = itn("Gall", [2, 16, 256, NH]); G0 = itn("G0", [2, 2, 2048, 128]); Gs = itn("Gs", [2, 2, 4, NH]); Gs0 = itn("Gs0", [2, 2, 4, 128])
    yL = itn("yL", [2, 512, NH]); y0L = itn("y0L", [512, 128])
    Gy = itn("Gy", [2, 4, 256, NH]); Gy0 = itn("Gy0", [2, 512, 128])

    Gall2 = Gall.ap.rearrange("g k p c -> g (k p) c")
    Gy2 = Gy.ap.rearrange("h k p c -> h (k p) c")
    PG = itn("PG", [16 * 256, NH]); PG0 = itn("PG0", [2048, 128]); SG = itn("SG", [2, 4, NH]); SG0 = itn("SG0", [4, 128])
    YG = itn("YG", [4 * 256, NH])

    def dsel(dst_ap, dst_t, src_fn, src_t):
        P.dma("sp", (dst_ap, dst_t), (src_fn, src_t), slot="sel_" + dst_t.res.name)

    def coll(in_ap, in_t, out_ap, out_t):
        P.op("pool", lambda e: e.collective_compute("AllGather", ALU.bypass, replica_groups=PAIRS, ins=[in_ap.opt()], outs=[out_ap.opt()]),
             reads=[in_t], writes=[out_t])

    def wscr(name, kind):
        npan, nk, npart, ncols = W_KINDS[kind]
        return P.dram(name, [npan, 128, nk * npart * ncols], BF16, "Internal")
    WS = {}
    jobs = []
    for l in range(DEPTH):
        for nm, src, kind in [("f1i", wf1i, 'ffn_in'), ("f1o", wf1o, 'ffn_out'), ("f2i", wf2i, 'ffn_in'), ("f2o", wf2o, 'ffn_out'),
                              ("wp", wp, 'proj'), ("wpt", wp, 'projt'), ("wo", wo, 'mix')]:
            WS[(nm, l)] = wscr(f"ws_{nm}{l}", kind)
            jobs.append((sub(src, src.ap[l]), WS[(nm, l)], kind))
    emit_W(P, jobs)
    P.barrier(); P.reset()

    def pm_dst(row, c0, n):
        if c0 == 0:
            return (pm0L.ap[row:row + 128, 0:64], pm0L)
        return (pmL.ap[row:row + 128, c0 - 64:c0 - 64 + n], pmL)

    def psc_dst(c0, n):
        if c0 == 0:
            return (psc0L.ap[0:8, 0:64], psc0L)
        return (pscL.ap[0:8, c0 - 64:c0 - 64 + n], pscL)

    def y_src(kc, c0, n):
        g, k = kc // 4, kc % 4
        if c0 == 0:
            return (Gy0.ap[g, k * 128:(k + 1) * 128, 0:64], Gy0)
        return (YG.ap[k * 256 + g * 128:k * 256 + (g + 1) * 128, c0 - 64:c0 - 64 + n], YG)

    def m_src(row, nrows, c, n):
        if c < 64:
            return (PG0.ap[row:row + nrows, c:c + n], PG0)
        h, cl = (c - 64) // NH, (c - 64) % NH
        kk, ro = row // 128, row % 128
        return (PG.ap[kk * 256 + h * 128 + ro:kk * 256 + h * 128 + ro + nrows, cl:cl + n], PG)

    sc_pieces = [(0, 1, (SG0.ap[:, 0:64], SG0))]
    for h in range(2):
        for q in range(NH // 1024):
            sc_pieces.append((1 + h * 64 + q * 16, 16, (SG.ap[h, :, q * 1024:(q + 1) * 1024], SG)))

    def ydst(row, c0, n):
        if c0 == 0:
            return [(y0L.ap[row:row + 128, 0:64], y0L)]
        h, cl = (c0 - 64) // NH, (c0 - 64) % NH
        return [(yL.ap[h, row:row + 128, cl:cl + n], yL)]

    def exchange_proj():
        P.barrier()
        for k in range(32):
            coll(pmL.ap[k * 128:(k + 1) * 128, :], pmL, Gall.ap[k // 16, k % 16], Gall)
        coll(pm0L.ap, pm0L, G0.ap.rearrange("r g p c -> (r g p) c"), G0)
        coll(pscL.ap, pscL, Gs.ap.rearrange("r g p c -> (r g p) c"), Gs)
        coll(psc0L.ap, psc0L, Gs0.ap.rearrange("r g p c -> (r g p) c"), Gs0)
        P.barrier()
        for q in range(4):
            dsel(PG.ap[q * 1024:(q + 1) * 1024, :], PG, (lambda val, q=q: Gall2[ds(val, 1), q * 1024:(q + 1) * 1024, :]), Gall)
        dsel(PG0.ap, PG0, (lambda val: G0.ap[0][ds(val, 1), :, :]), G0)
        for h in range(2):
            dsel(SG.ap[h], SG, (lambda val, h=h: Gs.ap[h][ds(val, 1), :, :]), Gs)
        dsel(SG0.ap, SG0, (lambda val: Gs0.ap[0][ds(val, 1), :, :]), Gs0)
        P.barrier(); P.reset()

    def exchange_y():
        P.barrier()
        for h in range(2):
            for kk in range(4):
                coll(yL.ap[h, kk * 128:(kk + 1) * 128, :], yL, Gy.ap[h, kk], Gy)
        coll(y0L.ap, y0L, Gy0.ap.rearrange("r p c -> (r p) c"), Gy0)
        P.barrier()
        for q in range(2):
            dsel(YG.ap[q * 512:(q + 1) * 512, :], YG, (lambda val, q=q: Gy2[ds(val, 1), q * 512:(q + 1) * 512, :]), Gy)
        P.barrier(); P.reset()

    def mixers(l):
        dd = {"src": m_src, "sc_pieces": sc_pieces, "ydst": ydst, "mp": sub(mp, mp.ap[l]), "cm": sub(cm, cm.ap[l]), "rt": rt}
        emit_M(P, dd, m_tiles, {"g": [0, 128], "w": 256, "r": 384}, ltot=L)

    base = {"pm_dst": pm_dst, "psc_dst": psc_dst, "y_src": y_src}
    dd = dict(base); dd.update({"hin": xT, "hout": hA, "ffn1_i": WS[("f1i", 0)], "ffn1_o": WS[("f1o", 0)], "wp": WS[("wp", 0)], "wpt": WS[("wpt", 0)],
                                "lnp": sub(lnp, lnp.ap[0])})
    emit_T(P, dd, ['ffn1', 'proj'], t_tiles)
    hcur, hnext = hA, hB
    for l in range(DEPTH):
        exchange_proj()
        mixers(l)
        exchange_y()
        dd = dict(base); dd.update({"hin": hcur, "w_out": WS[("wo", l)], "ffn2_i": WS[("f2i", l)], "ffn2_o": WS[("f2o", l)], "lnp": sub(lnp, lnp.ap[l + 1])})
        if l + 1 < DEPTH:
            dd.update({"hout": hnext, "ffn1_i": WS[("f1i", l + 1)], "ffn1_o": WS[("f1o", l + 1)], "wp": WS[("wp", l + 1)], "wpt": WS[("wpt", l + 1)]})
            emit_T(P, dd, ['mix', 'ffn2', 'ffn1', 'proj'], t_tiles)
            hcur, hnext = hnext, hcur
        else:
            dd["hout"] = out
            emit_T(P, dd, ['mix', 'ffn2'], t_tiles)
    P.emit()
    return nc


def wout_perm():
    rows = []
    for g in range(2):
        rows += list(range((2 * g) * 128, (2 * g + 1) * 128)) + list(range((2 * g + 1) * 128, (2 * g + 2) * 128))
        rows += list(range(512 + g * 128, 512 + (g + 1) * 128)) + list(range(768 + g * 128, 768 + (g + 1) * 128))
    return np.array(rows)


def host_inputs(z, b, r):
    f32 = np.float32
    c = np.ascontiguousarray
    tok = np.concatenate([np.zeros((48, 1024), f32), z['meta_tokens'], z['x'][b, r * NH:(r + 1) * NH]], 0)
    lnp = np.stack([lnp_pack([z['ln_g'][0, 0], z['ln_g'][0, 1], z['ln_g'][0, 2]], [z['ln_b'][0, 0], z['ln_b'][0, 1], z['ln_b'][0, 2]]),
                    lnp_pack([z['ln_g'][1, 0], z['ln_g'][0, 1], z['ln_g'][0, 2]], [z['ln_b'][1, 0], z['ln_b'][0, 1], z['ln_b'][0, 2]]),
                    lnp_pack([z['ln_g'][1, 0], z['ln_g'][1, 1], z['ln_g'][1, 2]], [z['ln_b'][1, 0], z['ln_b'][1, 1], z['ln_b'][1, 2]])], 0)
    return {"xT": c(tok.T), "lnp": lnp, "gsel": np.array([[r]], np.int32),
            "mp": np.stack([mp_pack(z, l, r) for l in range(DEPTH)], 0), "cm": np.stack([cm_pack(z, l, r) for l in range(DEPTH)], 0)}


def shared_inputs(z):
    c = np.ascontiguousarray
    cols = proj_cols()
    return {"w_ff1_in": z['w_ff1_in'], "w_ff1_out": z['w_ff1_out'], "w_ff2_in": z['w_ff2_in'], "w_ff2_out": z['w_ff2_out'],
            "wp": c(z['w_in'][:, :, cols]), "w_out": c(z['w_out'][:, wout_perm(), :]), "rt": rt_pack(L)}


_NC = {}


def kernel(x, meta_tokens, ln_g, ln_b, w_ff1_in, w_ff1_out, w_ff2_in, w_ff2_out, w_in, w_out,
           gdn_conv_w, gdn_a_log, gdn_dt_bias, gdn_norm_w, rwkv_mu, rwkv_w0, rwkv_w_up,
           rwkv_a0, rwkv_a_up, rwkv_g_up, rwkv_k_k, rwkv_k_a, rwkv_r_k, rwkv_lnx_g, rwkv_lnx_b,
           ret_norm_g, ret_norm_b):
    z = dict(x=x, meta_tokens=meta_tokens, ln_g=ln_g, ln_b=ln_b, w_ff1_in=w_ff1_in, w_ff1_out=w_ff1_out,
             w_ff2_in=w_ff2_in, w_ff2_out=w_ff2_out, w_in=w_in, w_out=w_out, gdn_conv_w=gdn_conv_w,
             gdn_a_log=gdn_a_log, gdn_dt_bias=gdn_dt_bias, gdn_norm_w=gdn_norm_w, rwkv_mu=rwkv_mu,
             rwkv_w0=rwkv_w0, rwkv_w_up=rwkv_w_up, rwkv_a0=rwkv_a0, rwkv_a_up=rwkv_a_up, rwkv_g_up=rwkv_g_up,
             rwkv_k_k=rwkv_k_k, rwkv_k_a=rwkv_k_a, rwkv_r_k=rwkv_r_k, rwkv_lnx_g=rwkv_lnx_g,
             rwkv_lnx_b=rwkv_lnx_b, ret_norm_g=ret_norm_g, ret_norm_b=ret_norm_b)
    z = {k: np.ascontiguousarray(np.asarray(v, dtype=np.float32)) for k, v in z.items()}
    B = z['x'].shape[0]
    if 'nc' not in _NC:
        _NC['nc'] = build_all()
    sh = shared_inputs(z)
    maps = []
    for core in range(8):
        m = dict(sh)
        m.update(host_inputs(z, core // 2, core % 2))
        maps.append(m)
    res = run_bass_kernel_spmd(_NC['nc'], maps, core_ids=list(range(8))).results
    out = np.zeros((B, 2 * NH, 1024), np.float32)
    for core in range(8):
        b, r = core // 2, core % 2
        out[b, r * NH:(r + 1) * NH] = res[core]["out"][:, 64:].T
    return out
```

```python
import numpy as np
import concourse.bass as bass
import concourse.mybir as mybir
from concourse.bass_utils import run_bass_kernel_spmd

F32 = mybir.dt.float32
BF16 = mybir.dt.bfloat16
AF = mybir.ActivationFunctionType
ALU = mybir.AluOpType


class Res:
    __slots__ = ("name", "w", "rd")

    def __init__(self, name):
        self.name = name
        self.w = None
        self.rd = {}


class T:
    def __init__(self, ap, res):
        self.ap = ap
        self.res = res

    def __getitem__(self, k):
        return self.ap[k]


class Prog:
    ENGS = ["pe", "act", "dve", "pool", "sp"]

    def __init__(self, nc, a32=53000, a16=0):
        self.nc = nc
        self.ops = {e: [] for e in self.ENGS}
        self.cnt = {e: 0 for e in self.ENGS}
        self.seen = {e: {} for e in self.ENGS}
        self.slots = {}
        self.ctx = []
        self.nalloc = 0
        self.a32, self.a16 = a32, a16
        cm = nc.sbuf_tensor("A32", [128, a32], F32); self.A32 = cm.__enter__(); self.ctx.append(cm)
        self.o32 = self.o16 = 0
        self.dynsel = None
        self.dynval = None
        self.banks = [self.ps([128, 512], F32, f"bank{i}") for i in range(8)]
        self.bi = 0

    def psum(self):
        b = self.banks[self.bi % 8]
        self.bi += 1
        return b

    def reset(self):
        self.o32 = self.o16 = 0

    def barrier(self):
        targets = [(e, self.cnt[e]) for e in self.ENGS[:4] if self.cnt[e] > 0] + list(self.slots.items())
        for e in self.ENGS:
            waits = [(k, v) for k, v in targets if self.seen[e].get(k, 0) < v]
            for k, v in waits:
                self.seen[e][k] = v
            if waits:
                self.ops[e].append((waits, None, None))

    def sb(self, shape, dt=F32, name=None):
        self.nalloc += 1
        name = name or f"t{self.nalloc}"
        parts = shape[0]
        n = 1
        for s_ in shape[1:]:
            n *= s_
        if dt == F32:
            off = self.o32; self.o32 += n
            assert self.o32 <= self.a32, ("A32 overflow", name, self.o32)
            ap = self.A32[0:parts, off:off + n]
        else:
            n2 = (n + 1) // 2
            off = self.o32; self.o32 += n2
            assert self.o32 <= self.a32, ("A32 overflow", name, self.o32)
            ap = self.A32[0:parts, off:off + n2].bitcast(dt)
            assert tuple(ap.shape) == (parts, n2 * 2), ap.shape
            ap = ap[:, 0:n]
        if len(shape) == 3:
            ap = ap.rearrange("p (a b) -> p a b", a=shape[1])
        elif len(shape) == 4:
            ap = ap.rearrange("p (a b c) -> p a b c", a=shape[1], b=shape[2])
        return T(ap, Res(name))

    def ps(self, shape, dt=F32, name=None):
        self.nalloc += 1
        name = name or f"p{self.nalloc}"
        cm = self.nc.psum_tensor(name, list(shape), dt)
        t = cm.__enter__()
        self.ctx.append(cm)
        return T(t, Res(name))

    def dram(self, name, shape, dt, kind):
        t = self.nc.dram_tensor(name, list(shape), dt, kind=kind)
        return T(t.ap(), Res(name))

    def op(self, eng, fn, reads=(), writes=(), dma=None):
        waits = {}
        seen = self.seen[eng]

        def need(tok, raw):
            semkey, val, weng = tok
            if weng == eng and semkey == eng:
                if eng == "pe" or not raw:
                    return
                if dma is None and False:
                    return
            if seen.get(semkey, 0) >= val:
                return
            waits[semkey] = max(waits.get(semkey, 0), val)

        def flat(lst):
            o = []
            for r in lst:
                r = r.res if isinstance(r, T) else r
                if isinstance(r, (list, tuple)):
                    o.extend(r)
                else:
                    o.append(r)
            return o
        reads = flat(reads)
        writes = flat(writes)
        for r in reads:
            if r.w is not None:
                need(r.w, True)
        for r in writes:
            for tok in r.rd.values():
                need(tok, dma is not None)
            if r.w is not None:
                need(r.w, dma is not None)
        for k, v in waits.items():
            seen[k] = v
        if dma is not None:
            self.slots[dma] = self.slots.get(dma, 0) + 16
            tok = (dma, self.slots[dma], eng)
            inc = (dma, 16)
        else:
            self.cnt[eng] += 1
            tok = (eng, self.cnt[eng], eng)
            inc = (eng, 1)
        for r in writes:
            r.w = tok
            r.rd = {}
        for r in reads:
            old = r.rd.get(tok[0])
            if old is None or old[1] < tok[1]:
                r.rd[tok[0]] = tok
        self.ops[eng].append((list(waits.items()), fn, inc))

    def mm(self, out, lhsT, rhs, start=True, stop=True, reads=(), writes=None, **kw):
        o_ap, o_r = out
        l_ap, l_r = lhsT
        r_ap, r_r = rhs
        self.op("pe", lambda e: e.matmul(o_ap, l_ap, r_ap, start=start, stop=stop, **kw),
                reads=[l_r, r_r], writes=[o_r])

    def dma(self, eng, out, in_, slot=None, **kw):
        o_ap, o_r = out
        i_ap, i_r = in_
        if callable(i_ap):
            assert eng == "sp"
            fn_ap = i_ap
            if slot is None:
                rr_ = o_r.res if isinstance(o_r, T) else o_r
                if isinstance(rr_, (list, tuple)):
                    rr_ = rr_[0]
                slot = "d_" + rr_.name
            self.op(eng, lambda e: e.dma_start(out=o_ap, in_=fn_ap(self.dynval), **kw), reads=[i_r], writes=[o_r], dma=slot)
            return
        if slot is None:
            rr_ = o_r.res if isinstance(o_r, T) else o_r
            if isinstance(rr_, (list, tuple)):
                rr_ = rr_[0]
            slot = "d_" + rr_.name
        self.op(eng, lambda e: e.dma_start(out=o_ap, in_=i_ap, **kw), reads=[i_r], writes=[o_r], dma=slot)

    def emit(self):
        nc = self.nc
        semnames = list(self.ENGS[:4]) + list(self.slots.keys())
        sems = {}
        for n in semnames:
            cm = nc.semaphore("s_" + n)
            sems[n] = cm.__enter__()
            self.ctx.append(cm)
        fin = [(k, v) for k, v in self.slots.items()] + [(e, self.cnt[e]) for e in self.ENGS[:4] if self.cnt[e] > 0]
        prog = self

        def run(engname, e):
            if engname == "sp" and prog.dynsel is not None:
                e.reg_load(dreg, prog.dynsel)
                prog.dynval = e.snap(dreg)
            for waits, fn, inc in prog.ops[engname]:
                for k, v in waits:
                    e.wait_ge(sems[k], v)
                if fn is None:
                    continue
                ins = fn(e)
                ins.then_inc(sems[inc[0]], inc[1])
            if engname == "sp":
                for k, v in fin:
                    e.wait_ge(sems[k], v)

        dreg = None
        if prog.dynsel is not None:
            cm = nc.sync.register("dynsel_r")
            dreg = cm.__enter__()
            self.ctx.append(cm)
        with nc.Block() as block:
            @block.tensor
            def _(e):
                run("pe", e)

            @block.scalar
            def _(e):
                run("act", e)

            @block.vector
            def _(e):
                run("dve", e)

            @block.gpsimd
            def _(e):
                run("pool", e)

            @block.sync
            def _(e):
                run("sp", e)
        for cm in reversed(self.ctx):
            cm.__exit__(None, None, None)


ALPHA = 4.0 ** 0.25
LN_EPS = 1e-5
D_FF = 2816
NPROJ = 4104


class TState:
    pass


class TCtx:
    pass


def emit_T(P, d, stages, tiles):
    st = TState()
    NC_ = 2
    cx = []
    for i in range(NC_):
        c = TCtx()
        c.hT = P.sb([128, 8, 512], F32, f"hT{i}")
        c.hb = P.sb([128, 8, 512], BF16, f"hb{i}")
        c.r = P.sb([128, 8, 512], F32, f"r{i}")
        c.act = P.sb([128, 22, 512], BF16, f"act{i}")
        cx.append(c)
    st.hbm = P.sb([128, 8, 64], BF16, "hbm")
    st.ystg = P.sb([128, 8, 512], F32, "ystg")
    st.rsqb = P.sb([128, 8, 512], BF16, "rsqb")
    st.rb = P.sb([128, 8, 512], BF16, "rb")
    st.onesb = P.sb([128, 128], BF16, "onesb")
    NB = 3
    st.wbf = [P.sb([128, 4096], BF16, f"wbf{i}") for i in range(NB)]
    st.wi = 0
    st.sg = [P.sb([128, 512], F32, f"sg{i}") for i in range(2)]
    st.sgi = 0
    st.pout = [P.sb([128, 512], F32, f"pout{i}") for i in range(3)]
    st.pi = 0
    st.mean = P.sb([128, 512], F32, "mean")
    st.msq = P.sb([128, 512], F32, "msq")
    st.var = P.sb([128, 512], F32, "var")
    st.rstd = P.sb([128, 512], F32, "rstd")
    st.lnp = P.sb([128, 48], F32, "lnp_s")
    st.ones = P.sb([128, 128], F32, "ones")
    psum = P.psum

    P.dma("sp", (st.lnp[:], st.lnp), (d['lnp'][:], d['lnp']))
    P.op("pool", lambda e: e.memset(st.ones[:], 1.0), writes=[st.ones])
    P.op("pool", lambda e: e.tensor_copy(out=st.onesb[:], in_=st.ones[:]), reads=[st.ones], writes=[st.onesb])

    def load_wb(wt, panel, nk, npart, ncols):
        i = st.wi % NB
        st.wi += 1
        wb = st.wbf[i]
        tot = nk * npart * ncols
        P.dma("sp", (wb[:, :tot], wb), (wt[panel], wt))
        return wb[:, :tot].rearrange("p (k a c) -> p k a c", k=nk, a=npart), wb

    def layernorm(c, li):
        hT, n, r, hb = c.hT, c.n, c.r, c.hb
        eps = LN_EPS / ALPHA ** 2
        g = st.lnp[:, li * 16: li * 16 + 8]
        b = st.lnp[:, li * 16 + 8: li * 16 + 16]
        rsq, rb = st.rsqb, st.rb
        P.op("act", lambda e: e.activation(out=rsq[:, :, :n], in_=r[:, :, :n], func=AF.Square), reads=[r], writes=[rsq])
        P.op("dve", lambda e: e.tensor_copy(out=rb[:, :, :n], in_=r[:, :, :n]), reads=[r], writes=[rb])
        ps1 = psum(); ps2 = psum()
        for dc in range(8):
            P.mm((ps1[:, :n], ps1), (st.onesb[:], st.onesb), (rb[:, dc, :n], rb), start=dc == 0, stop=dc == 7)
        for dc in range(8):
            P.mm((ps2[:, :n], ps2), (st.onesb[:], st.onesb), (rsq[:, dc, :n], rsq), start=dc == 0, stop=dc == 7)
        mean, msq, var, rstd = st.mean, st.msq, st.var, st.rstd
        P.op("act", lambda e: e.activation(out=mean[:, :n], in_=ps1[:, :n], func=AF.Copy, scale=1.0 / 1024), reads=[ps1], writes=[mean])
        P.op("dve", lambda e: e.tensor_tensor(out=msq[:, :n], in0=mean[:, :n], in1=mean[:, :n], op=ALU.mult), reads=[mean], writes=[msq])
        P.op("dve", lambda e: e.scalar_tensor_tensor(out=var[:, :n], in0=ps2[:, :n], scalar=1.0 / 1024, op0=ALU.mult, in1=msq[:, :n], op1=ALU.subtract),
             reads=[ps2, msq], writes=[var])
        P.op("dve", lambda e: e.tensor_scalar(out=var[:, :n], in0=var[:, :n], scalar1=eps, scalar2=None, op0=ALU.add), reads=[var], writes=[var])
        P.op("act", lambda e: e.activation(out=var[:, :n], in_=var[:, :n], func=AF.Ln), reads=[var], writes=[var])
        P.op("act", lambda e: e.activation(out=rstd[:, :n], in_=var[:, :n], func=AF.Exp, scale=-0.5), reads=[var], writes=[rstd])
        P.op("dve", lambda e: e.tensor_tensor(out=r[:, :, :n], in0=r[:, :, :n], in1=mean[:, :n].unsqueeze(1).to_broadcast([128, 8, n]), op=ALU.subtract),
             reads=[r, mean], writes=[r])
        P.op("dve", lambda e: e.tensor_tensor(out=r[:, :, :n], in0=r[:, :, :n], in1=rstd[:, :n].unsqueeze(1).to_broadcast([128, 8, n]), op=ALU.mult),
             reads=[r, rstd], writes=[r])
        for dc in range(8):
            P.op("act", lambda e, dc=dc: e.activation(out=r[:, dc, :n], in_=r[:, dc, :n], func=AF.Identity, scale=g[:, dc:dc + 1], bias=b[:, dc:dc + 1]),
                 reads=[r, st.lnp], writes=[r])
        P.op("dve", lambda e: e.tensor_copy(out=hb[:, :, :n], in_=r[:, :, :n]), reads=[r], writes=[hb])
        c.hT, c.r = c.r, c.hT

    def ffn(cs, wi, wo, li):
        for pp in range(11):
            wv, wb = load_wb(wi, pp, 8, 2, 256)
            for j in range(2):
                for c in cs:
                    n = c.n
                    pg = psum(); pu = psum()
                    for kc in range(8):
                        P.mm((pg[:, :n], pg), (wv[:, kc, 0, j * 128:(j + 1) * 128], wb), (c.hb[:, kc, :n], c.hb), start=kc == 0, stop=kc == 7)
                    for kc in range(8):
                        P.mm((pu[:, :n], pu), (wv[:, kc, 1, j * 128:(j + 1) * 128], wb), (c.hb[:, kc, :n], c.hb), start=kc == 0, stop=kc == 7)
                    sg = st.sg[st.sgi % 2]; st.sgi += 1
                    P.op("act", lambda e, sg=sg, pg=pg, n=n: e.activation(out=sg[:, :n], in_=pg[:, :n], func=AF.Silu), reads=[pg], writes=[sg])
                    fc = 2 * pp + j
                    P.op("dve", lambda e, sg=sg, pu=pu, fc=fc, c=c, n=n: e.tensor_tensor(out=c.act[:, fc, :n], in0=sg[:, :n], in1=pu[:, :n], op=ALU.mult),
                         reads=[sg, pu], writes=[c.act])
        for dp in range(8):
            wv, wb = load_wb(wo, dp, 22, 1, 128)
            for c in cs:
                n = c.n
                po = psum()
                for kc in range(22):
                    P.mm((po[:, :n], po), (wv[:, kc, 0, :], wb), (c.act[:, kc, :n], c.act), start=kc == 0, stop=kc == 21)
                P.op("dve", lambda e, po=po, dp=dp, r_=c.r, h_=c.hT, n=n: e.scalar_tensor_tensor(out=r_[:, dp, :n], in0=po[:, :n], scalar=0.5 / ALPHA, op0=ALU.mult,
                                                                                          in1=h_[:, dp, :n], op1=ALU.add), reads=[po, c.hT], writes=[c.r])
        for c in cs:
            layernorm(c, li)

    def mixout(cs, li):
        for c in cs:
            n = c.n
            for kc in range(8):
                P.dma("sp", (st.ystg[:, kc, :n], st.ystg), d['y_src'](kc, c.c0, n))
            P.op("dve", lambda e, c=c, n=n: e.tensor_copy(out=c.act[:, 0:8, :n], in_=st.ystg[:, :, :n]), reads=[st.ystg], writes=[c.act])
        for dp in range(8):
            wv, wb = load_wb(d['w_out'], dp, 8, 1, 128)
            for c in cs:
                n = c.n
                po = psum()
                for kc in range(8):
                    P.mm((po[:, :n], po), (wv[:, kc, 0, :], wb), (c.act[:, kc, :n], c.act), start=kc == 0, stop=kc == 7)
                P.op("dve", lambda e, po=po, dp=dp, r_=c.r, h_=c.hT, n=n: e.scalar_tensor_tensor(out=r_[:, dp, :n], in0=po[:, :n], scalar=1.0 / ALPHA, op0=ALU.mult,
                                                                                          in1=h_[:, dp, :n], op1=ALU.add), reads=[po, c.hT], writes=[c.r])
        for c in cs:
            layernorm(c, li)

    def proj(cs):
        for c in cs:
            if c.chunk0:
                P.op("pool", lambda e, c=c: e.tensor_copy(out=st.hbm[:, :, :], in_=c.hb[:, :, :64]), reads=[c.hb], writes=[st.hbm])
                P.op("pool", lambda e: e.memset(st.hbm[:, :, 0:48], 0.0), writes=[st.hbm])

        def rhs(c, kc):
            if c.chunk0:
                return (st.hbm[:, kc, :c.n], st.hbm)
            return (c.hb[:, kc, :c.n], c.hb)
        for pp in range(16):
            wv, wb = load_wb(d['wp'], pp, 8, 1, 256)
            for j in range(2):
                for c in cs:
                    n = c.n
                    po = psum()
                    for kc in range(8):
                        P.mm((po[:, :n], po), (wv[:, kc, 0, j * 128:(j + 1) * 128], wb), rhs(c, kc), start=kc == 0, stop=kc == 7)
                    pt = st.pout[st.pi % 3]; st.pi += 1
                    P.op("act", lambda e, pt=pt, po=po, n=n: e.activation(out=pt[:, :n], in_=po[:, :n], func=AF.Copy), reads=[po], writes=[pt])
                    row = (2 * pp + j) * 128
                    P.dma("pool", d['pm_dst'](row, c.c0, n), (pt[:, :n], pt), slot="o_" + pt.res.name)
        wv, wb = load_wb(d['wpt'], 0, 8, 1, 8)
        for c in cs:
            n = c.n
            po = psum()
            for kc in range(8):
                P.mm((po[0:8, :n], po), (wv[:, kc, 0, :], wb), rhs(c, kc), start=kc == 0, stop=kc == 7)
            pt = st.pout[st.pi % 3]; st.pi += 1
            P.op("act", lambda e, pt=pt, po=po, n=n: e.activation(out=pt[0:8, :n], in_=po[0:8, :n], func=AF.Copy), reads=[po], writes=[pt])
            P.dma("pool", d['psc_dst'](c.c0, n), (pt[0:8, :n], pt), slot="o_" + pt.res.name)

    hv = d['hin'][:].rearrange("(k p) t -> p k t", p=128)
    ov = d['hout'][:].rearrange("(k p) t -> p k t", p=128)
    for t0 in range(0, len(tiles), NC_):
        cs = []
        for i, (c0, n, chunk0) in enumerate(tiles[t0:t0 + NC_]):
            c = cx[i]
            c.c0, c.n, c.chunk0 = c0, n, chunk0
            c.hT, c.r = c.r, c.hT
            P.dma("sp", (c.hT[:, :, :n], c.hT), (hv[:, :, c0:c0 + n], d['hin']))
            if stages[0] != 'mix':
                P.op("dve", lambda e, hb_=c.hb, h_=c.hT, n=n: e.tensor_copy(out=hb_[:, :, :n], in_=h_[:, :, :n]), reads=[c.hT], writes=[c.hb])
            cs.append(c)
        for s in stages:
            if s == 'mix':
                mixout(cs, 1)
            elif s == 'ffn2':
                ffn(cs, d['ffn2_i'], d['ffn2_o'], 2)
            elif s == 'ffn1':
                ffn(cs, d['ffn1_i'], d['ffn1_o'], 0)
            elif s == 'proj':
                proj(cs)
        for c in cs:
            P.dma("pool", (ov[:, :, c.c0:c.c0 + c.n], d['hout']), (c.hT[:, :, :c.n], c.hT), slot="o_" + c.hT.res.name)


W_KINDS = {
    'ffn_in': (11, 8, 2, 256), 'ffn_out': (8, 22, 1, 128), 'proj': (16, 8, 1, 256), 'projt': (1, 8, 1, 8), 'mix': (8, 8, 1, 128)}


def emit_W(P, jobs):
    NB = 4
    stg = [P.sb([128, 4096], F32, f"wstg{i}") for i in range(NB)]
    wbf = [P.sb([128, 4096], BF16, f"wcv{i}") for i in range(NB)]
    cnt = 0
    engs = ["pool", "act", "dve"]
    for src, dst, kind in jobs:
        npan, nk, npart, ncols = W_KINDS[kind]
        tot = nk * npart * ncols
        sv = src[:].rearrange("(k p) f -> p k f", p=128)
        for pn in range(npan):
            s32, s16 = stg[cnt % NB], wbf[cnt % NB]
            v = s32[:, :tot].rearrange("p (k a c) -> p k a c", k=nk, a=npart)
            if kind == 'ffn_in':
                colsl = [slice(pn * 256, (pn + 1) * 256), slice(D_FF + pn * 256, D_FF + (pn + 1) * 256)]
            elif kind == 'projt':
                colsl = [slice(4096, 4104)]
            else:
                colsl = [slice(pn * ncols, (pn + 1) * ncols)]
            for a, cs in enumerate(colsl):
                P.dma("sp", (v[:, :, a, :], s32), (sv[:, :, cs], src))
            eng = engs[cnt % 3]
            if eng == "act":
                P.op("act", lambda e, s16=s16, s32=s32, tot=tot: e.activation(out=s16[:, :tot], in_=s32[:, :tot], func=AF.Copy), reads=[s32], writes=[s16])
            else:
                P.op(eng, lambda e, s16=s16, s32=s32, tot=tot: e.tensor_copy(out=s16[:, :tot], in_=s32[:, :tot]), reads=[s32], writes=[s16])
            P.dma("pool", (dst[pn], dst), (s16[:, :tot], s16), slot="o_" + s16.res.name)
            cnt += 1


L = 8256
NCH = 129


DBG = False


def emit_M(P, d, tiles, yrows, do=('gdn', 'rwkv', 'ret'), ltot=L):
    nchtot = ltot // 64

    mp = P.sb([128, 64], F32, "mp_s")
    cm = P.sb([128, 13, 128], F32, "cm_s")
    P.dma("sp", (mp[:], mp), (d['mp'][:], d['mp']))
    P.dma("sp", (cm[:], cm), (d['cm'][:], d['cm']))
    I128 = cm[:, 0, :]
    I64 = cm[0:64, 0, 0:64]
    UT = cm[0:64, 1, 0:64]
    SLm = cm[0:64, 2, 0:64]
    SU = cm[0:64, 3, 0:64]
    ONES = cm[:, 4, :]
    BLK = cm[:, 5, :]
    LOWUP = cm[:, 6, :]
    QDT = cm[0:64, 8, 0:64]; KDT = cm[0:64, 8, 64:128]
    HMc = [cm[0:64, 9, :], cm[0:64, 10, :]]
    HMr = [cm[0:64, 11, 0:64], cm[0:64, 12, 0:64]]
    DTj = [cm[0:64, 7, 0:64], cm[0:64, 7, 64:128]]

    NF, NQ, NR = 32, 26, 8
    FPt = P.sb([128, NF * 512], F32, "FP"); fres = [Res(f"F{i}") for i in range(NF)]
    QPt = P.sb([64, NQ * 512], F32, "QP"); qres = [Res(f"Q{i}") for i in range(NQ)]
    RPt = P.sb([64, NR * 1024], F32, "RP"); rres = [Res(f"R{i}") for i in range(NR)]

    def Fv(i, k=1, shape=None, parts=128):
        ap = FPt.ap[0:parts, i * 512:(i + k) * 512]
        if shape is not None:
            a, b_ = shape
            ap = ap[:, 0:a * b_].rearrange("p (a b) -> p a b", a=a)
        elif k > 1:
            ap = ap.rearrange("p (a b) -> p a b", a=k)
        return T(ap, fres[i:i + k])

    def Qv(i):
        return T(QPt.ap[:, i * 512:(i + 1) * 512].rearrange("p (c f) -> p c f", f=64), [qres[i]])

    def Rv(i):
        return T(RPt.ap[:, i * 1024:(i + 1) * 1024].rearrange("p (c f) -> p c f", f=128), [rres[i]])


    dbgs = {}

    def dbg(name, ap, t):
        shape = list(ap.shape)
        dd = P.dram("dbg_" + name, shape, F32, "ExternalOutput")
        P.dma("pool", (dd[:], dd), (ap, t), slot="dbg_" + name)
        dbgs[name] = dd

    psum = P.psum

    def run_rr(gens):
        gens = list(gens)
        while gens:
            for gn in list(gens):
                try:
                    next(gn)
                except StopIteration:
                    gens.remove(gn)

    def ld(dst_ap, dst_T, row, nrows, c0, n, halo=0):
        P.dma("sp", (dst_ap[:, halo:halo + n], dst_T), d['src'](row, nrows, c0, n))
        if halo:
            if c0 == 0:
                P.op("pool", lambda e: e.memset(dst_ap[:, 0:halo], 0.0), writes=[dst_T])
            else:
                P.dma("sp", (dst_ap[:, 0:halo], dst_T), d['src'](row, nrows, c0 - halo, halo), allow_slow_non_contiguous=True)

    def bc_mid(ap, k):
        p, f = ap.shape
        return ap.unsqueeze(1).to_broadcast([p, k, f])

    def bc_in(ap, f):
        p, k = ap.shape
        return ap.unsqueeze(2).to_broadcast([p, k, f])

    def tt(eng, out, in0, in1, op, reads, writes):
        P.op(eng, lambda e: e.tensor_tensor(out=out, in0=in0, in1=in1, op=op), reads=reads, writes=writes)

    def act(out, in_, func, reads, writes, **kw):
        P.op("act", lambda e: e.activation(out=out, in_=in_, func=func, **kw), reads=reads, writes=writes)

    def mm(out, o_r, lhsT, l_r, rhs, r_r, start=True, stop=True):
        P.op("pe", lambda e: e.matmul(out, lhsT, rhs, start=start, stop=stop), reads=[l_r, r_r], writes=[o_r])

    def tr(out, o_r, in_, i_r, ident):
        P.op("pe", lambda e: e.transpose(out, in_, ident), reads=[i_r, cm], writes=[o_r])

    inv_t = {}

    def inv_tiles(key):
        if key not in inv_t:
            b0 = {"inv0": 12, "inv1": 18}[key]
            inv_t[key] = dict(A=[Qv(b0), Qv(b0 + 1)], AT=[Qv(b0 + 2), Qv(b0 + 3)], Pm=[Qv(b0 + 4), Qv(b0 + 5)])
        return inv_t[key]

    def inverse(key, X, XT, nch):
        tl = inv_tiles(key)
        Pm = tl['Pm'][0]
        tt("pool", Pm[:, :nch, :], bc_mid(I64, nch), X[:, :nch, :], ALU.subtract, [cm, X], [Pm])
        A, AT = X, XT
        for lvl in range(5):
            A2, A2T = tl['A'][lvl % 2], tl['AT'][lvl % 2]
            b2 = psum()
            for c in range(nch):
                mm(b2[0:64, c * 64:(c + 1) * 64], b2, A[:, c, :], A, AT[:, c, :], AT)
            if lvl < 4:
                b1 = psum()
                for c in range(nch):
                    mm(b1[0:64, c * 64:(c + 1) * 64], b1, AT[:, c, :], AT, A[:, c, :], A)
            P.op("dve", lambda e, A2T=A2T, b2=b2: e.tensor_copy(out=A2T[:, :nch, :], in_=b2[0:64, 0:nch * 64].rearrange("p (c f) -> p c f", f=64)),
                 reads=[b2], writes=[A2T])
            if lvl < 4:
                act(A2[:, :nch, :], b1[0:64, 0:nch * 64].rearrange("p (c f) -> p c f", f=64), AF.Copy, [b1], [A2])
            yield
            b3 = psum()
            for c in range(nch):
                mm(b3[0:64, c * 64:(c + 1) * 64], b3, A2T[:, c, :], A2T, Pm[:, c, :], Pm)
            Pn = tl['Pm'][(lvl + 1) % 2]
            tt("dve", Pn[:, :nch, :], b3[0:64, 0:nch * 64].rearrange("p (c f) -> p c f", f=64), Pm[:, :nch, :], ALU.add, [b3, Pm], [Pn])
            A, AT, Pm = A2, A2T, Pn
            yield
        return Pm

    MP_GC = 0
    MP_NW = 24
    MP_AL = 25
    MP_DT = 27
    if 'gdn' in do:
        g = {}
        scs = [P.sb([4, 1024], F32, f"g_sc{i}") for i in range(2)]
        scT = P.sb([64, nchtot, 4], F32, "g_scT")
        for pi, (c0, k, scsrc) in enumerate(d['sc_pieces']):
            sc = scs[pi % 2]
            b = psum()
            P.dma("sp", (sc[:, 0:k * 64], sc), scsrc)
            for c in range(k):
                tr(b[0:64, c * 4:(c + 1) * 4], b, sc[0:4, c * 64:(c + 1) * 64], sc, cm[0:4, 0, 0:4])
            act(scT[:, c0:c0 + k, :], b[0:64, 0:k * 4].rearrange("p (c f) -> p c f", f=4), AF.Copy, [b], [scT])
        g['beta'] = []; g['lg'] = []; g['neG'] = []; g['eGL'] = []; g['egl'] = []
        nea = P.sb([64, 2], F32, "g_nea")
        act(nea[:], mp[0:64, MP_AL:MP_AL + 2], AF.Exp, [mp], [nea])
        P.op("dve", lambda e: e.tensor_scalar(out=nea[:], in0=nea[:], scalar1=-1.0, scalar2=None, op0=ALU.mult), reads=[nea], writes=[nea])
        for j in range(2):
            beta = P.sb([64, nchtot], F32, f"g_beta{j}")
            lg = P.sb([64, nchtot], F32, f"g_lg{j}")
            neG = P.sb([64, nchtot], F32, f"g_neG{j}")
            eGL = P.sb([64, nchtot], F32, f"g_eGL{j}")
            egl = P.sb([128, nchtot], F32, f"g_egl{j}")
            act(beta[:], scT[:, :, j], AF.Sigmoid, [scT], [beta])
            act(lg[:], scT[:, :, 2 + j], AF.Exp, [scT, mp], [lg], bias=mp[0:64, MP_DT + j:MP_DT + j + 1])
            P.op("dve", lambda e, lg=lg: e.tensor_scalar(out=lg[:], in0=lg[:], scalar1=1.0, scalar2=None, op0=ALU.add), reads=[lg], writes=[lg])
            act(lg[:], lg[:], AF.Ln, [lg], [lg])
            P.op("dve", lambda e, lg=lg, j=j: e.tensor_scalar(out=lg[:], in0=lg[:], scalar1=nea[:, j:j + 1], scalar2=None, op0=ALU.mult), reads=[lg, nea], writes=[lg])
            b = psum()
            mm(b[0:64, 0:nchtot], b, UT, cm, lg[:], lg)
            act(neG[:], b[0:64, 0:nchtot], AF.Exp, [b], [neG])
            P.op("dve", lambda e, neG=neG: e.tensor_scalar(out=neG[:], in0=neG[:], scalar1=-1.0, scalar2=None, op0=ALU.mult), reads=[neG], writes=[neG])
            b = psum()
            mm(b[0:64, 0:nchtot], b, SLm, cm, lg[:], lg)
            act(eGL[:], b[0:64, 0:nchtot], AF.Exp, [b], [eGL])
            b = psum()
            mm(b[:, 0:nchtot], b, ONES[0:64, :], cm, lg[:], lg)
            act(egl[:], b[:, 0:nchtot], AF.Exp, [b], [egl])
            g['beta'].append(beta); g['lg'].append(lg); g['neG'].append(neG); g['eGL'].append(eGL); g['egl'].append(egl)
        gs = []
        for j in range(2):
            G = {}
            fb = 16 * j
            G['raw'] = Fv(fb + 0, 4, shape=(3, 515))
            G['cs'] = Fv(fb + 4, 3); G['z'] = Fv(fb + 7); G['sq'] = Fv(fb + 8, 2); G['kqn'] = Fv(fb + 10, 2)
            G['wpnT'] = Fv(fb + 12); G['qdec'] = Fv(fb + 13); G['oT'] = Fv(fb + 14); G['o2'] = Fv(fb + 15)
            qsl = [0, 1, 2, 3, 4, 5, 6] if j == 0 else [7, 8, 9, 10, 11, 24, 25]
            for qi, nm in zip(qsl, ['LGS', 'dec', 'decS', 'decI', 'X', 'XT', 'qkT']):
                G[nm] = Qv(qi)
            for ri, nm in enumerate(['kgn', 'kd', 'vtm', 'lgB']):
                G[nm] = Rv(4 * j + ri)
            G['vn'] = P.sb([64, 128], F32, f"g_vn{j}")
            gs.append(G)
        g['S'] = [[P.sb([128, 128], F32, f"g_S{j}_{i}") for i in range(2)] for j in range(2)]
        g['si'] = [0, 0]
        for j in range(2):
            P.op("pool", lambda e, j=j: e.memset(g['S'][j][0][:], 0.0), writes=[g['S'][j][0]])

    def gdn_gen(j, c0, nch, first):
        n = nch * 64
        cg0 = c0 // 64
        G = gs[j]
        raw, z, cs, sq, kqn = G['raw'], G['z'], G['cs'], G['sq'], G['kqn']
        beta, lg, neG, eGL, egl = g['beta'][j], g['lg'][j], g['neG'][j], g['eGL'][j], g['egl'][j]
        for ty in range(3):
            ld(raw[:, ty, 0:3 + n], raw, j * 512 + ty * 128, 128, c0, n, halo=3)
        ld(z[:, :n], z, j * 512 + 384, 128, c0, n)
        for ty in range(3):
            cwl = [mp[:, MP_GC + j * 12 + ty * 4 + tap: MP_GC + j * 12 + ty * 4 + tap + 1] for tap in range(4)]
            cw = lambda tap, cwl=cwl: cwl[tap]
            P.op("dve", lambda e, ty=ty, cw=cw: e.tensor_scalar(out=cs[:, ty, :n], in0=raw[:, ty, 0:n], scalar1=cw(0), scalar2=None, op0=ALU.mult),
                 reads=[raw, mp], writes=[cs])
            for tap in range(1, 4):
                P.op("dve", lambda e, ty=ty, cw=cw, tap=tap: e.scalar_tensor_tensor(out=cs[:, ty, :n], in0=raw[:, ty, tap:tap + n], scalar=cw(tap), op0=ALU.mult,
                                                                                 in1=cs[:, ty, :n], op1=ALU.add), reads=[raw, mp, cs], writes=[cs])
        act(cs[:, :, :n], cs[:, :, :n], AF.Silu, [cs], [cs])
        yield
        tt("pool", sq[:, :, :n], cs[:, 0:2, :n], cs[:, 0:2, :n], ALU.mult, [cs], [sq])
        for a in range(2):
            b = psum()
            mm(b[:, :n], b, ONES, cm, sq[:, a, :n], sq)
            P.op("dve", lambda e, b=b, a=a: e.tensor_scalar(out=sq[:, a, :n], in0=b[:, :n], scalar1=1e-6, scalar2=None, op0=ALU.add), reads=[b], writes=[sq])
        act(sq[:, :, :n], sq[:, :, :n], AF.Ln, [sq], [sq])
        act(sq[:, :, :n], sq[:, :, :n], AF.Exp, [sq], [sq], scale=-0.5)
        tt("dve", kqn[:, 0, :n], cs[:, 1, :n], sq[:, 1, :n], ALU.mult, [cs, sq], [kqn])
        P.op("dve", lambda e: e.scalar_tensor_tensor(out=kqn[:, 1, :n], in0=cs[:, 0, :n], scalar=128.0 ** -0.5, op0=ALU.mult, in1=sq[:, 0, :n], op1=ALU.mult),
             reads=[cs, sq], writes=[kqn])
        yield
        LGS, dec, decS, decI, X, XT, qkT = G['LGS'], G['dec'], G['decS'], G['decI'], G['X'], G['XT'], G['qkT']
        kgn, kd, vtm, wpnT, lgB, qdec, oT = G['kgn'], G['kd'], G['vtm'], G['wpnT'], G['lgB'], G['qdec'], G['oT']
        lgs = lg[:, cg0:cg0 + nch]
        tt("pool", LGS[:, :nch, :], bc_mid(SLm, nch), bc_in(lgs, 64), ALU.mult, [cm, lg], [LGS])
        b = psum()
        for c in range(nch):
            mm(b[0:64, c * 64:(c + 1) * 64], b, LGS[:, c, :], LGS, UT, cm)
        act(dec[:, :nch, :], b[0:64, 0:n].rearrange("p (c f) -> p c f", f=64), AF.Exp, [b], [dec])
        tt("pool", decS[:, :nch, :], dec[:, :nch, :], bc_mid(SU, nch), ALU.mult, [dec, cm], [decS])
        tt("pool", decS[:, :nch, :], decS[:, :nch, :], bc_in(beta[:, cg0:cg0 + nch], 64), ALU.mult, [decS, beta], [decS])
        tt("pool", decI[:, :nch, :], dec[:, :nch, :], bc_mid(UT, nch), ALU.mult, [dec, cm], [decI])
        yield
        bkk = psum(); bkq = psum()
        for c in range(nch):
            ks = kqn[:, 0, c * 64:(c + 1) * 64]
            mm(bkk[0:64, c * 64:(c + 1) * 64], bkk, ks, kqn, ks, kqn)
            mm(bkq[0:64, c * 64:(c + 1) * 64], bkq, ks, kqn, kqn[:, 1, c * 64:(c + 1) * 64], kqn)
        tt("dve", X[:, :nch, :], bkk[0:64, 0:n].rearrange("p (c f) -> p c f", f=64), decS[:, :nch, :], ALU.mult, [bkk, decS], [X])
        tt("dve", qkT[:, :nch, :], bkq[0:64, 0:n].rearrange("p (c f) -> p c f", f=64), decI[:, :nch, :], ALU.mult, [bkq, decI], [qkT])
        yield
        b = psum()
        for c in range(nch):
            tr(b[0:64, c * 64:(c + 1) * 64], b, X[:, c, :], X, I64)
        act(XT[:, :nch, :], b[0:64, 0:n].rearrange("p (c f) -> p c f", f=64), AF.Copy, [b], [XT])
        yield
        T2T = yield from inverse(f"inv{j}", X, XT, nch)
        yield
        for h0 in range(0, nch, 4):
            k4 = min(4, nch - h0)
            b = psum()
            for c in range(k4):
                tr(b[0:64, c * 128:(c + 1) * 128], b, kqn[:, 0, (h0 + c) * 64:(h0 + c + 1) * 64], kqn, I128)
            bv = b[0:64, 0:k4 * 128].rearrange("p (c f) -> p c f", f=128)
            tt("dve", kgn[:, h0:h0 + k4, :], bv, bc_in(neG[:, cg0 + h0:cg0 + h0 + k4], 128), ALU.mult, [b, neG], [kgn])
            tt("dve", kd[:, h0:h0 + k4, :], bv, bc_in(eGL[:, cg0 + h0:cg0 + h0 + k4], 128), ALU.mult, [b, eGL], [kd])
            b = psum()
            for c in range(k4):
                tr(b[0:64, c * 128:(c + 1) * 128], b, cs[:, 2, (h0 + c) * 64:(h0 + c + 1) * 64], cs, I128)
            act(vtm[:, h0:h0 + k4, :], b[0:64, 0:k4 * 128].rearrange("p (c f) -> p c f", f=128), AF.Copy, [b], [vtm])
            yield
        b = psum()
        for c in range(nch):
            mm(b[:, c * 64:(c + 1) * 64], b, kgn[:, c, :], kgn, T2T[:, c, :], T2T)
        act(wpnT[:, :n], b[:, :n], AF.Copy, [b], [wpnT])
        yield
        tt("pool", lgB[:, :nch, :], bc_mid(ONES[0:64, :], nch), bc_in(lgs, 128), ALU.mult, [cm, lg], [lgB])
        b = psum()
        for c in range(nch):
            mm(b[:, c * 64:(c + 1) * 64], b, lgB[:, c, :], lgB, UT, cm)
        act(qdec[:, :n], b[:, :n], AF.Exp, [b], [qdec])
        tt("dve", qdec[:, :n], qdec[:, :n], kqn[:, 1, :n], ALU.mult, [qdec, kqn], [qdec])

        def step(c):
            cg = cg0 + c
            S = g['S'][j][g['si'][j] % 2]
            Sn = g['S'][j][(g['si'][j] + 1) % 2]
            g['si'][j] += 1
            vn = G['vn']
            pv = psum()
            mm(pv[0:64, 0:128], pv, T2T[:, c, :], T2T, vtm[:, c, :], vtm, start=True, stop=False)
            mm(pv[0:64, 0:128], pv, wpnT[:, c * 64:(c + 1) * 64], wpnT, S[:], S, start=False, stop=True)
            act(vn[:], pv[0:64, 0:128], AF.Identity, [pv, beta], [vn], scale=beta[:, cg:cg + 1])
            yield
            po = psum()
            mm(po[:, 0:64], po, S[:], S, qdec[:, c * 64:(c + 1) * 64], qdec, start=True, stop=False)
            mm(po[:, 0:64], po, vn[:], vn, qkT[:, c, :], qkT, start=False, stop=True)
            act(oT[:, c * 64:(c + 1) * 64], po[:, 0:64], AF.Copy, [po], [oT])
            pS = psum()
            mm(pS[:, 0:128], pS, kd[:, c, :], kd, vn[:], vn)
            P.op("dve", lambda e: e.scalar_tensor_tensor(out=Sn[:], in0=S[:], scalar=egl[:, cg:cg + 1], op0=ALU.mult, in1=pS[:, 0:128], op1=ALU.add),
                 reads=[S, egl, pS], writes=[Sn])
            yield

        def post():
            o2 = G['o2']
            tt("pool", o2[:, :n], oT[:, :n], oT[:, :n], ALU.mult, [oT], [o2])
            b = psum()
            mm(b[:, :n], b, ONES, cm, o2[:, :n], o2)
            P.op("dve", lambda e: e.tensor_scalar(out=o2[:, :n], in0=b[:, :n], scalar1=1.0 / 128, scalar2=LN_EPS, op0=ALU.mult, op1=ALU.add), reads=[b], writes=[o2])
            act(o2[:, :n], o2[:, :n], AF.Ln, [o2], [o2])
            act(o2[:, :n], o2[:, :n], AF.Exp, [o2], [o2], scale=-0.5)
            tt("dve", oT[:, :n], oT[:, :n], o2[:, :n], ALU.mult, [oT, o2], [oT])
            act(z[:, :n], z[:, :n], AF.Silu, [z], [z])
            P.op("dve", lambda e: e.scalar_tensor_tensor(out=oT[:, :n], in0=oT[:, :n], scalar=mp[:, MP_NW:MP_NW + 1], op0=ALU.mult, in1=z[:, :n], op1=ALU.mult),
                 reads=[oT, mp, z], writes=[oT])
            for dst in d['ydst'](yrows['g'][j], c0, n):
                P.dma("pool", dst, (oT[:, :n], oT), slot=f"o_g_oT{j}")
        yield
        for c in range(nch):
            yield from step(c)
        post()


    MU0 = 29; W0 = 33; A0 = 34; KKc = 35; KAc = 36; LNG = 37; LNB = 38; RKc = 39; RNG = 40; RNB = 41; HM0 = 42; GAM = 44
    col = lambda cidx: mp[:, cidx:cidx + 1]

    def ts(eng, out, in0, s1, op0, reads, writes, s2=None, op1=None):
        if op1 is None:
            P.op(eng, lambda e: e.tensor_scalar(out=out, in0=in0, scalar1=s1, scalar2=None, op0=op0), reads=reads, writes=writes)
        else:
            P.op(eng, lambda e: e.tensor_scalar(out=out, in0=in0, scalar1=s1, scalar2=s2, op0=op0, op1=op1), reads=reads, writes=writes)

    def stt(eng, out, in0, sc_, op0, in1, op1, reads, writes):
        P.op(eng, lambda e: e.scalar_tensor_tensor(out=out, in0=in0, scalar=sc_, op0=op0, in1=in1, op1=op1), reads=reads, writes=writes)

    def v3(b, nch_, f, p=64):
        return b[0:p, 0:nch_ * f].rearrange("p (c f) -> p c f", f=f)

    if 'rwkv' in do:
        w = {}
        w['raw'] = Fv(0, 5, shape=(4, 513))
        w['pf'] = Fv(5, 4)
        for fi, nm in enumerate(['lrt', 'sgw', 'a', 'gT', 'kk', 'kp', 'bs', 't1', 't2', 'Gd', 'eA', 'rt', 'atp', 'btm0', 'btm1', 'ktm0', 'ktm1', 'bh', 'kh', 'WT0', 'WT1', 'yT']):
            w[nm] = Fv(9 + fi)
        for qi, nm in enumerate(['X0', 'X1', 'XT0', 'XT1', 'Aak0', 'Aak1', 'Arb0', 'Arb1', 'Ark0', 'Ark1', 'M10', 'M11']):
            w[nm] = Qv(qi)
        for ri, nm in enumerate(['Vp0', 'Vp1', 'Atm0', 'Atm1', 'Bh0', 'Bh1', 'Kh0', 'Kh1']):
            w[nm] = Rv(ri)
        w['U'] = P.sb([64, 2, 128], F32, "w_U")
        w['GC'] = P.sb([128, 8], F32, "w_GC")
        w['omk'] = P.sb([128, 1], F32, "w_omk")
        w['rmask'] = P.sb([128, 512], F32, "w_rmask")
        w['S'] = [P.sb([128, 128], F32, f"w_S{i}") for i in range(2)]
        w['si'] = 0
        P.op("pool", lambda e: e.memset(w['S'][0][:], 0.0), writes=[w['S'][0]])
        P.op("pool", lambda e: e.memset(w['rmask'][:], 1.0), writes=[w['rmask']])
        P.op("pool", lambda e: e.memset(w['rmask'][:].rearrange("p (c f) -> p c f", f=64)[:, :, 0:1], 0.0), writes=[w['rmask']])
        ts("dve", w['omk'][:], col(KAc), -1.0, ALU.mult, [mp], [w['omk']], s2=1.0, op1=ALU.add)

    def rwkv_tile(c0, nch, first):
        n = nch * 64
        raw, pf = w['raw'], w['pf']
        for ty in range(4):
            ld(raw[:, ty, 0:1 + n], raw, 1024 + ty * 128, 128, c0, n, halo=1)
        tt("dve", pf[:, :, :n], raw[:, :, 0:n], raw[:, :, 1:1 + n], ALU.subtract, [raw], [pf])
        for ty in range(4):
            stt("dve", pf[:, ty, :n], pf[:, ty, :n], col(MU0 + ty), ALU.mult, raw[:, ty, 1:1 + n], ALU.add, [pf, mp, raw], [pf])
        r_, k_, v_ = pf[:, 0, :n], pf[:, 1, :n], pf[:, 2, :n]
        lrt, sgw, a_, gT, kk, kp, bs, t1, t2, Gd, eA = [w[x] for x in ['lrt', 'sgw', 'a', 'gT', 'kk', 'kp', 'bs', 't1', 't2', 'Gd', 'eA']]
        act(lrt[0:32, :n], pf[0:32, 3, :n], AF.Tanh, [pf], [lrt])
        act(lrt[32:64, :n], pf[32:64, 3, :n], AF.Copy, [pf], [lrt])
        act(lrt[64:128, :n], pf[64:128, 3, :n], AF.Sigmoid, [pf], [lrt])
        b = psum(); mm(b[:, :n], b, LOWUP[0:32, :], cm, lrt[0:32, :n], lrt)
        act(sgw[:, :n], b[:, :n], AF.Sigmoid, [b, mp], [sgw], bias=col(W0))
        b = psum(); mm(b[:, :n], b, LOWUP[32:64, :], cm, lrt[32:64, :n], lrt)
        act(a_[:, :n], b[:, :n], AF.Sigmoid, [b, mp], [a_], bias=col(A0))
        b = psum(); mm(b[:, :n], b, LOWUP[64:128, :], cm, lrt[64:128, :n], lrt)
        act(gT[:, :n], b[:, :n], AF.Copy, [b], [gT])
        ts("dve", kk[:, :n], k_, col(KKc), ALU.mult, [pf, mp], [kk])
        tt("pool", t1[:, :n], kk[:, :n], kk[:, :n], ALU.mult, [kk], [t1])
        b = psum(); mm(b[:, :n], b, BLK, cm, t1[:, :n], t1)
        ts("dve", t1[:, :n], b[:, :n], 1e-6, ALU.add, [b], [t1])
        act(t1[:, :n], t1[:, :n], AF.Ln, [t1], [t1])
        act(t1[:, :n], t1[:, :n], AF.Exp, [t1], [t1], scale=-0.5)
        tt("dve", kk[:, :n], kk[:, :n], t1[:, :n], ALU.mult, [kk, t1], [kk])
        ts("dve", t2[:, :n], a_[:, :n], col(KAc), ALU.mult, [a_, mp, w['omk']], [t2], s2=w['omk'][:, 0:1], op1=ALU.add)
        tt("dve", kp[:, :n], k_, t2[:, :n], ALU.mult, [pf, t2], [kp])
        tt("pool", bs[:, :n], kk[:, :n], a_[:, :n], ALU.mult, [kk, a_], [bs])
        ts("dve", sgw[:, :n], sgw[:, :n], -0.6065306597126334, ALU.mult, [sgw], [sgw])
        P.op("dve", lambda e: e.tensor_tensor_scan(out=Gd[:, :n], data0=w['rmask'][:, :n], data1=sgw[:, :n], initial=0.0, op0=ALU.mult, op1=ALU.add),
             reads=[w['rmask'], sgw], writes=[Gd])
        rt_, atp, bh, kh = w['rt'], w['atp'], w['bh'], w['kh']
        act(eA[:, :n], Gd[:, :n], AF.Exp, [Gd], [eA])
        tt("dve", rt_[:, :n], r_, eA[:, :n], ALU.mult, [pf, eA], [rt_])
        tt("pool", t1[:, :n], Gd[:, :n], sgw[:, :n], ALU.subtract, [Gd, sgw], [t1])
        act(eA[:, :n], t1[:, :n], AF.Exp, [t1], [eA])
        tt("dve", atp[:, :n], kk[:, :n], eA[:, :n], ALU.mult, [kk, eA], [atp])
        act(eA[:, :n], Gd[:, :n], AF.Exp, [Gd], [eA], scale=-1.0)
        for j in range(2):
            stt("dve", w[f'btm{j}'][:, :n], bs[:, :n], col(HM0 + j), ALU.mult, eA[:, :n], ALU.mult, [bs, mp, eA], [w[f'btm{j}']])
            stt("dve", w[f'ktm{j}'][:, :n], kp[:, :n], col(HM0 + j), ALU.mult, eA[:, :n], ALU.mult, [kp, mp, eA], [w[f'ktm{j}']])
        Gd3 = Gd[:, :n].rearrange("p (c f) -> p c f", f=64)
        last = Gd3[:, :, 63]
        tt("pool", t1[:, :n].rearrange("p (c f) -> p c f", f=64), bc_in(last, 64), Gd3, ALU.subtract, [Gd], [t1])
        act(eA[:, :n], t1[:, :n], AF.Exp, [t1], [eA])
        tt("dve", bh[:, :n], bs[:, :n], eA[:, :n], ALU.mult, [bs, eA], [bh])
        tt("dve", kh[:, :n], kp[:, :n], eA[:, :n], ALU.mult, [kp, eA], [kh])
        GC = w['GC']
        act(GC[:, :nch], last, AF.Exp, [Gd], [GC])
        for h0 in range(0, nch, 4):
            k4 = min(4, nch - h0)
            def trb(src, sr):
                b = psum()
                for c in range(k4):
                    tr(b[0:64, c * 128:(c + 1) * 128], b, src[:, (h0 + c) * 64:(h0 + c + 1) * 64], sr, I128)
                return b
            b = trb(v_, pf)
            for j in range(2):
                tt("dve", w[f'Vp{j}'][:, h0:h0 + k4, :], v3(b, k4, 128), bc_mid(HMc[j], k4), ALU.mult, [b, cm], [w[f'Vp{j}']])
            b = trb(atp, atp)
            for j in range(2):
                stt("dve", w[f'Atm{j}'][:, h0:h0 + k4, :], v3(b, k4, 128), -1.0, ALU.mult, bc_mid(HMc[j], k4), ALU.mult, [b, cm], [w[f'Atm{j}']])
            b = trb(bh, bh)
            for j in range(2):
                tt("dve", w[f'Bh{j}'][:, h0:h0 + k4, :], v3(b, k4, 128), bc_mid(HMc[j], k4), ALU.mult, [b, cm], [w[f'Bh{j}']])
            b = trb(kh, kh)
            for j in range(2):
                tt("dve", w[f'Kh{j}'][:, h0:h0 + k4, :], v3(b, k4, 128), bc_mid(HMc[j], k4), ALU.mult, [b, cm], [w[f'Kh{j}']])
        TT = [None, None]

        def head_gen(j):
            btm, ktm = w[f'btm{j}'], w[f'ktm{j}']
            X, XT, Aak, Arb, Ark = w[f'X{j}'], w[f'XT{j}'], w[f'Aak{j}'], w[f'Arb{j}'], w[f'Ark{j}']
            def grp(lh, lr_, rh, rr):
                b = psum()
                for c in range(nch):
                    mm(b[0:64, c * 64:(c + 1) * 64], b, lh[:, c * 64:(c + 1) * 64], lr_, rh[:, c * 64:(c + 1) * 64], rr)
                return b
            b = grp(btm, btm, atp, atp)
            tt("dve", X[:, :nch, :], v3(b, nch, 64), bc_mid(SU, nch), ALU.mult, [b, cm], [X])
            yield
            b = grp(atp, atp, btm, btm)
            tt("dve", XT[:, :nch, :], v3(b, nch, 64), bc_mid(SLm, nch), ALU.mult, [b, cm], [XT])
            yield
            b = grp(atp, atp, ktm, ktm)
            stt("dve", Aak[:, :nch, :], v3(b, nch, 64), -1.0, ALU.mult, bc_mid(SLm, nch), ALU.mult, [b, cm], [Aak])
            yield
            b = grp(btm, btm, rt_, rt_)
            tt("dve", Arb[:, :nch, :], v3(b, nch, 64), bc_mid(UT, nch), ALU.mult, [b, cm], [Arb])
            yield
            b = grp(ktm, ktm, rt_, rt_)
            tt("dve", Ark[:, :nch, :], v3(b, nch, 64), bc_mid(UT, nch), ALU.mult, [b, cm], [Ark])
            yield
            TTj = yield from inverse(f"inv{j}", X, XT, nch)
            TT[j] = TTj
            Atm, WT, Aak, M1 = w[f'Atm{j}'], w[f'WT{j}'], w[f'Aak{j}'], w[f'M1{j}']
            b = psum()
            for c in range(nch):
                mm(b[:, c * 64:(c + 1) * 64], b, Atm[:, c, :], Atm, TT[j][:, c, :], TT[j])
            act(WT[:, :n], b[:, :n], AF.Copy, [b], [WT])
            yield
            b = psum()
            for c in range(nch):
                mm(b[0:64, c * 64:(c + 1) * 64], b, Aak[:, c, :], Aak, TT[j][:, c, :], TT[j])
            act(M1[:, :nch, :], v3(b, nch, 64), AF.Copy, [b], [M1])
        run_rr([head_gen(0), head_gen(1)])
        yT = w['yT']

        def step(c):
            S = w['S'][w['si'] % 2]; Sn = w['S'][(w['si'] + 1) % 2]; w['si'] += 1
            U = w['U']
            cs_ = slice(c * 64, (c + 1) * 64)
            pu = psum()
            for j in range(2):
                mm(pu[0:64, j * 128:(j + 1) * 128], pu, w[f'WT{j}'][:, cs_], w[f'WT{j}'], S[:], S, start=True, stop=False)
                mm(pu[0:64, j * 128:(j + 1) * 128], pu, w[f'M1{j}'][:, c, :], w[f'M1{j}'], w[f'Vp{j}'][:, c, :], w[f'Vp{j}'], start=False, stop=True)
            act(U[:], pu[0:64, 0:256].rearrange("p (j f) -> p j f", f=128), AF.Copy, [pu], [U])
            py = psum()
            mm(py[:, 0:64], py, S[:], S, rt_[:, cs_], rt_, start=True, stop=False)
            for j in range(2):
                mm(py[:, 0:64], py, U[:, j, :], U, w[f'Arb{j}'][:, c, :], w[f'Arb{j}'], start=False, stop=False)
                mm(py[:, 0:64], py, w[f'Vp{j}'][:, c, :], w[f'Vp{j}'], w[f'Ark{j}'][:, c, :], w[f'Ark{j}'], start=False, stop=(j == 1))
            act(yT[:, cs_], py[:, 0:64], AF.Copy, [py], [yT])
            pS = psum()
            for j in range(2):
                mm(pS[:, 0:128], pS, w[f'Bh{j}'][:, c, :], w[f'Bh{j}'], U[:, j, :], U, start=(j == 0), stop=False)
                mm(pS[:, 0:128], pS, w[f'Kh{j}'][:, c, :], w[f'Kh{j}'], w[f'Vp{j}'][:, c, :], w[f'Vp{j}'], start=False, stop=(j == 1))
            stt("dve", Sn[:], S[:], GC[:, c:c + 1], ALU.mult, pS[:, 0:128], ALU.add, [S, GC, pS], [Sn])

        def post():
            b = psum(); mm(b[:, :n], b, BLK, cm, yT[:, :n], yT)
            stt("dve", yT[:, :n], b[:, :n], -1.0 / 64, ALU.mult, yT[:, :n], ALU.add, [b, yT], [yT])
            tt("pool", t1[:, :n], yT[:, :n], yT[:, :n], ALU.mult, [yT], [t1])
            b = psum(); mm(b[:, :n], b, BLK, cm, t1[:, :n], t1)
            ts("dve", t1[:, :n], b[:, :n], 1.0 / 64, ALU.mult, [b], [t1], s2=64e-5, op1=ALU.add)
            act(t1[:, :n], t1[:, :n], AF.Ln, [t1], [t1])
            act(t1[:, :n], t1[:, :n], AF.Exp, [t1], [t1], scale=-0.5)
            tt("dve", yT[:, :n], yT[:, :n], t1[:, :n], ALU.mult, [yT, t1], [yT])
            ts("dve", yT[:, :n], yT[:, :n], col(LNG), ALU.mult, [yT, mp], [yT], s2=col(LNB), op1=ALU.add)
            tt("pool", t2[:, :n], r_, kp[:, :n], ALU.mult, [pf, kp], [t2])
            ts("dve", t2[:, :n], t2[:, :n], col(RKc), ALU.mult, [t2, mp], [t2])
            b = psum(); mm(b[:, :n], b, BLK, cm, t2[:, :n], t2)
            tt("dve", t2[:, :n], b[:, :n], v_, ALU.mult, [b, pf], [t2])
            tt("pool", yT[:, :n], yT[:, :n], t2[:, :n], ALU.add, [yT, t2], [yT])
            tt("dve", yT[:, :n], yT[:, :n], gT[:, :n], ALU.mult, [yT, gT], [yT])
            for dst in d['ydst'](yrows['w'], c0, n):
                P.dma("pool", dst, (yT[:, :n], yT), slot="o_w_yT")
        return step, post

    if 'ret' in do:
        rr = {}
        for fi, nm in enumerate(['QA', 'KA', 'QB', 'KB', 'COS', 'SIN', 'qr', 'kr', 'qd', 'kdc', 't']):
            rr[nm] = Fv(fi, parts=64)
        for fi, nm in enumerate(['v', 'gate', 'oT', 't1', 'tA', 'tB']):
            rr[nm] = Fv(11 + fi)
        for qi, nm in enumerate(['qk0', 'qk1', 'Kd0', 'Kd1']):
            rr[nm] = Qv(qi)
        for ri, nm in enumerate(['Vp0', 'Vp1']):
            rr[nm] = Rv(ri)
        rr['S'] = [P.sb([64, 128], F32, f"r_S{i}") for i in range(2)]
        rr['si'] = 0
        P.op("pool", lambda e: e.memset(rr['S'][0][:], 0.0), writes=[rr['S'][0]])

    def ret_tile(c0, nch):
        n = nch * 64
        QA, KA, QB, KB, COS, SIN, qr, kr, qd, kdc, t_ = [rr[x] for x in ['QA', 'KA', 'QB', 'KB', 'COS', 'SIN', 'qr', 'kr', 'qd', 'kdc', 't']]
        v_, gate, oT, t1 = rr['v'], rr['gate'], rr['oT'], rr['t1']
        base = 12 * 128
        for tl_, r0 in [(QA, base), (KA, base + 64), (QB, base + 128), (KB, base + 192)]:
            ld(tl_[:, :n], tl_, r0, 64, c0, n)
        ld(v_[:, :n], v_, base + 256, 128, c0, n)
        ld(gate[:, :n], gate, base + 384, 128, c0, n)
        P.dma("sp", (COS[:, :n], COS), (d['rt'][0, :, c0:c0 + n], d['rt']))
        P.dma("sp", (SIN[:, :n], SIN), (d['rt'][1, :, c0:c0 + n], d['rt']))
        tt("dve", qr[:, :n], QA[:, :n], COS[:, :n], ALU.mult, [QA, COS], [qr])
        tt("pool", t_[:, :n], QB[:, :n], SIN[:, :n], ALU.mult, [QB, SIN], [t_])
        tt("dve", qr[:, :n], qr[:, :n], t_[:, :n], ALU.add, [qr, t_], [qr])
        tt("dve", kr[:, :n], KA[:, :n], COS[:, :n], ALU.mult, [KA, COS], [kr])
        tt("pool", t_[:, :n], KB[:, :n], SIN[:, :n], ALU.mult, [KB, SIN], [t_])
        tt("dve", kr[:, :n], kr[:, :n], t_[:, :n], ALU.add, [kr, t_], [kr])
        q3 = lambda x: x[:, :n].rearrange("p (c f) -> p c f", f=64)
        tt("pool", q3(qd), q3(qr), bc_mid(QDT, nch), ALU.mult, [qr, cm], [qd])
        tt("pool", q3(kdc), q3(kr), bc_mid(KDT, nch), ALU.mult, [kr, cm], [kdc])
        for j in range(2):
            b = psum()
            for c in range(nch):
                mm(b[0:64, c * 64:(c + 1) * 64], b, kr[32 * j:32 * j + 32, c * 64:(c + 1) * 64], kr, qr[32 * j:32 * j + 32, c * 64:(c + 1) * 64], qr)
            tt("dve", rr[f'qk{j}'][:, :nch, :], v3(b, nch, 64), bc_mid(DTj[j], nch), ALU.mult, [b, cm], [rr[f'qk{j}']])
        for h0 in range(0, nch, 4):
            k4 = min(4, nch - h0)
            b = psum()
            for c in range(k4):
                tr(b[0:64, c * 128:(c + 1) * 128], b, v_[:, (h0 + c) * 64:(h0 + c + 1) * 64], v_, I128)
            for j in range(2):
                tt("dve", rr[f'Vp{j}'][:, h0:h0 + k4, :], v3(b, k4, 128), bc_mid(HMc[j], k4), ALU.mult, [b, cm], [rr[f'Vp{j}']])
        b = psum()
        for c in range(nch):
            tr(b[0:64, c * 64:(c + 1) * 64], b, kdc[:, c * 64:(c + 1) * 64], kdc, I64)
        for j in range(2):
            tt("dve", rr[f'Kd{j}'][:, :nch, :], v3(b, nch, 64), bc_mid(HMr[j], nch), ALU.mult, [b, cm], [rr[f'Kd{j}']])

        def step(c):
            S = rr['S'][rr['si'] % 2]; Sn = rr['S'][(rr['si'] + 1) % 2]; rr['si'] += 1
            cs_ = slice(c * 64, (c + 1) * 64)
            po = psum()
            mm(po[:, 0:64], po, S[:], S, qd[:, cs_], qd, start=True, stop=False)
            for j in range(2):
                mm(po[:, 0:64], po, rr[f'Vp{j}'][:, c, :], rr[f'Vp{j}'], rr[f'qk{j}'][:, c, :], rr[f'qk{j}'], start=False, stop=(j == 1))
            act(oT[:, cs_], po[:, 0:64], AF.Copy, [po], [oT])
            pS = psum()
            for j in range(2):
                mm(pS[0:64, 0:128], pS, rr[f'Kd{j}'][:, c, :], rr[f'Kd{j}'], rr[f'Vp{j}'][:, c, :], rr[f'Vp{j}'], start=(j == 0), stop=(j == 1))
            stt("dve", Sn[:], S[:], mp[0:64, GAM:GAM + 1], ALU.mult, pS[0:64, 0:128], ALU.add, [S, mp, pS], [Sn])

        def post():
            b = psum(); mm(b[:, :n], b, BLK, cm, oT[:, :n], oT)
            stt("dve", oT[:, :n], b[:, :n], -1.0 / 64, ALU.mult, oT[:, :n], ALU.add, [b, oT], [oT])
            tt("pool", t1[:, :n], oT[:, :n], oT[:, :n], ALU.mult, [oT], [t1])
            b = psum(); mm(b[:, :n], b, BLK, cm, t1[:, :n], t1)
            ts("dve", t1[:, :n], b[:, :n], 1.0 / 64, ALU.mult, [b], [t1], s2=LN_EPS, op1=ALU.add)
            act(t1[:, :n], t1[:, :n], AF.Ln, [t1], [t1])
            act(t1[:, :n], t1[:, :n], AF.Exp, [t1], [t1], scale=-0.5)
            tt("dve", oT[:, :n], oT[:, :n], t1[:, :n], ALU.mult, [oT, t1], [oT])
            ts("dve", oT[:, :n], oT[:, :n], col(RNG), ALU.mult, [oT, mp], [oT], s2=col(RNB), op1=ALU.add)
            act(gate[:, :n], gate[:, :n], AF.Silu, [gate], [gate])
            tt("dve", oT[:, :n], oT[:, :n], gate[:, :n], ALU.mult, [oT, gate], [oT])
            for dst in d['ydst'](yrows['r'], c0, n):
                P.dma("pool", dst, (oT[:, :n], oT), slot="o_r_oT")
        return step, post

    for ti, (c0, nch) in enumerate(tiles):
        if 'gdn' in do:
            run_rr([gdn_gen(0, c0, nch, ti == 0), gdn_gen(1, c0, nch, ti == 0)])
        if 'rwkv' in do:
            step, post = rwkv_tile(c0, nch, ti == 0)
            for c in range(nch):
                step(c)
            post()
        if 'ret' in do:
            step, post = ret_tile(c0, nch)
            for c in range(nch):
                step(c)
            post()


import numpy as np
GDN_QKV = 1536; D_A = 512; D_A_IN = 2056; D_B_IN = 896; D_C_IN = 768


def proj_cols():
    cols = []
    for g in range(2):
        for j in range(2):
            h = 2 * g + j
            cols += list(range(h * 128, (h + 1) * 128))
            cols += list(range(512 + h * 128, 512 + (h + 1) * 128))
            cols += list(range(1024 + h * 128, 1024 + (h + 1) * 128))
            cols += list(range(GDN_QKV + h * 128, GDN_QKV + (h + 1) * 128))
        o = D_A_IN
        cols += list(range(o + g * 128, o + (g + 1) * 128))
        cols += list(range(o + 256 + g * 128, o + 256 + (g + 1) * 128))
        cols += list(range(o + 512 + g * 128, o + 512 + (g + 1) * 128))
        cols += list(range(o + 768, o + 896))
        o = D_A_IN + D_B_IN
        q = [o + (2 * g + j) * 32 + i for j in range(2) for i in range(32)]
        k = [o + 128 + (2 * g + j) * 32 + i for j in range(2) for i in range(32)]
        sw = lambda lst: [lst[j * 32 + (i + 16) % 32] for j in range(2) for i in range(32)]
        cols += q + k
        cols += sw(q) + sw(k)
        cols += list(range(o + 256 + g * 128, o + 256 + (g + 1) * 128))
        cols += list(range(o + 512 + g * 128, o + 512 + (g + 1) * 128))
    for g in range(2):
        o = GDN_QKV + D_A
        cols += [o + 2 * g, o + 2 * g + 1, o + 4 + 2 * g, o + 4 + 2 * g + 1]
    assert len(cols) == 4104
    return np.array(cols)


def lnp_pack(gs, bs):
    out = np.zeros((128, 48), np.float32)
    for i in range(3):
        out[:, i * 16: i * 16 + 8] = gs[i].reshape(8, 128).T
        out[:, i * 16 + 8: i * 16 + 16] = bs[i].reshape(8, 128).T
    return out


import numpy as np


def cm_pack(z, l, g):
    cm = np.zeros((128, 13, 128), np.float32)
    cm[:, 0, :] = np.eye(128)
    i = np.arange(64)
    cm[:64, 1, :64] = (i[:, None] <= i[None, :])
    cm[:64, 2, :64] = (i[:, None] > i[None, :])
    cm[:64, 3, :64] = (i[:, None] < i[None, :])
    cm[:, 4, :] = 1.0
    cm[:64, 5, :64] = 1.0; cm[64:, 5, 64:] = 1.0
    cm[0:32, 6, :] = z['rwkv_w_up'][l][:, g * 128:(g + 1) * 128]
    cm[32:64, 6, :] = z['rwkv_a_up'][l][:, g * 128:(g + 1) * 128]
    cm[64:128, 6, :] = z['rwkv_g_up'][l][:, g * 128:(g + 1) * 128]
    for j in range(2):
        h = 2 * g + j
        lgam = np.log(1.0 - 2.0 ** (-5.0 - h))
        diff = i[None, :] - i[:, None]
        cm[:64, 7, j * 64:(j + 1) * 64] = np.where(diff >= 0, np.exp(lgam * np.maximum(diff, 0)), 0.0) * 32 ** -0.5
    for j in range(2):
        h = 2 * g + j
        lgam = np.log(1.0 - 2.0 ** (-5.0 - h))
        cm[j * 32:(j + 1) * 32, 8, 0:64] = np.exp(lgam * (i + 1.0))[None, :]
        cm[j * 32:(j + 1) * 32, 8, 64:128] = np.exp(lgam * (63.0 - i))[None, :] * 32 ** -0.5
        cm[:64, 9 + j, j * 64:(j + 1) * 64] = 1.0
        cm[:64, 11 + j, j * 32:(j + 1) * 32] = 1.0
    return cm


def mp_pack(z, l, g):
    mp = np.zeros((128, 64), np.float32)
    for j in range(2):
        h = 2 * g + j
        for ty in range(3):
            for tap in range(4):
                mp[:, j * 12 + ty * 4 + tap] = z['gdn_conv_w'][l][tap, ty * 512 + h * 128: ty * 512 + (h + 1) * 128]
        mp[:, 25 + j] = z['gdn_a_log'][l][h]
        mp[:, 27 + j] = z['gdn_dt_bias'][l][h]
    mp[:, 24] = z['gdn_norm_w'][l]
    sl = slice(g * 128, (g + 1) * 128)
    mu = z['rwkv_mu'][l]
    mp[:, 29] = mu[0:256][sl]; mp[:, 30] = mu[256:512][sl]; mp[:, 31] = mu[512:768][sl]; mp[:, 32] = mu[768:896]
    mp[:, 33] = z['rwkv_w0'][l][sl]; mp[:, 34] = z['rwkv_a0'][l][sl]; mp[:, 35] = z['rwkv_k_k'][l][sl]; mp[:, 36] = z['rwkv_k_a'][l][sl]
    mp[:, 37] = z['rwkv_lnx_g'][l][sl]; mp[:, 38] = z['rwkv_lnx_b'][l][sl]; mp[:, 39] = z['rwkv_r_k'][l].reshape(-1)[sl]
    mp[:, 40] = z['ret_norm_g'][l][sl]; mp[:, 41] = z['ret_norm_b'][l][sl]
    mp[0:64, 42] = 1.0; mp[64:128, 43] = 1.0
    for j in range(2):
        mp[j * 32:(j + 1) * 32, 44] = (1.0 - 2.0 ** (-5.0 - (2 * g + j))) ** 64
    return mp


def rt_pack(ltot):
    pos = np.arange(ltot, dtype=np.float32)
    inv = (1.0 / (10000.0 ** np.linspace(0.0, 1.0, 16, dtype=np.float32))).astype(np.float32)
    ang = pos[None, :] * inv[:, None]
    cos, sin = np.cos(ang), np.sin(ang)
    rt = np.zeros((2, 64, ltot), np.float32)
    for j in range(2):
        rt[0, j * 32:j * 32 + 16] = cos; rt[0, j * 32 + 16:j * 32 + 32] = cos
        rt[1, j * 32:j * 32 + 16] = -sin; rt[1, j * 32 + 16:j * 32 + 32] = sin
    return rt


DEPTH = 2
NH = 4096
NL = 64 + NH
PAIRS = [[0, 1], [2, 3], [4, 5], [6, 7]]
I32 = mybir.dt.int32


def build_all():
    t_tiles = [(0, 64, True)] + [(64 + i * 512, 512, False) for i in range(NH // 512)]
    m_tiles = [(0, 1)] + [(64 + i * 512, 8) for i in range(2 * NH // 512)]
    nc = bass.Bass("TRN2", target_bir_lowering=False)
    P = Prog(nc)
    ext = lambda name, shape: P.dram(name, shape, F32, "ExternalInput")
    itn = lambda name, shape: P.dram(name, shape, F32, "Internal")
    sub = lambda t, ap: T(ap, t.res)
    ds = bass.ds
    gsel = P.dram("gsel", [1, 1], I32, "ExternalInput")
    P.dynsel = gsel.ap[0:1, 0:1]
    xT = ext("xT", [1024, NL])
    lnp = ext("lnp", [3, 128, 48])
    wf1i = ext("w_ff1_in", [DEPTH, 1024, 2 * D_FF]); wf1o = ext("w_ff1_out", [DEPTH, D_FF, 1024])
    wf2i = ext("w_ff2_in", [DEPTH, 1024, 2 * D_FF]); wf2o = ext("w_ff2_out", [DEPTH, D_FF, 1024])
    wp = ext("wp", [DEPTH, 1024, NPROJ]); wo = ext("w_out", [DEPTH, 1024, 1024])
    mp = ext("mp", [DEPTH, 128, 64]); cm = ext("cm", [DEPTH, 128, 13, 128]); rt = ext("rt", [2, 64, L])
    out = P.dram("out", [1024, NL], F32, "ExternalOutput")
    hA = itn("hA", [1024, NL]); hB = itn("hB", [1024, NL])
    pmL = itn("pmL", [4096, NH]); pm0L = itn("pm0L", [4096, 128]); pscL = itn("pscL", [8, NH]); psc0L = itn("psc0L", [8, 128])
    Gall = itn("Gall", [2, 16, 256, NH]); G0 = itn("G0", [2, 2, 2048, 128]); Gs = itn("Gs", [2, 2, 4, NH]); Gs0 = itn("Gs0", [2, 2, 4, 128])
    yL = itn("yL", [2, 512, NH]); y0L = itn("y0L", [512, 128])
    Gy = itn("Gy", [2, 4, 256, NH]); Gy0 = itn("Gy0", [2, 512, 128])

    Gall2 = Gall.ap.rearrange("g k p c -> g (k p) c")
    Gy2 = Gy.ap.rearrange("h k p c -> h (k p) c")
    PGh = [itn(f"PGh{h}", [16 * 128, NH]) for h in range(2)]; PG0 = itn("PG0", [2048, 128]); SG = itn("SG", [2, 4, NH]); SG0 = itn("SG0", [4, 128])
    YG = itn("YG", [4 * 256, NH])

    def dsel(dst_ap, dst_t, src_fn, src_t):
        P.dma("sp", (dst_ap, dst_t), (src_fn, src_t), slot="sel_" + dst_t.res.name)

    def coll(in_ap, in_t, out_ap, out_t):
        P.op("pool", lambda e: e.collective_compute("AllGather", ALU.bypass, replica_groups=PAIRS, ins=[in_ap.opt()], outs=[out_ap.opt()]),
             reads=[in_t], writes=[out_t])

    def wscr(name, kind):
        npan, nk, npart, ncols = W_KINDS[kind]
        return P.dram(name, [npan, 128, nk * npart * ncols], BF16, "Internal")
    WS = {}
    jobs = []
    for l in range(DEPTH):
        for nm, src, kind in [("f1i", wf1i, 'ffn_in'), ("f1o", wf1o, 'ffn_out'), ("f2i", wf2i, 'ffn_in'), ("f2o", wf2o, 'ffn_out'),
                              ("wp", wp, 'proj'), ("wpt", wp, 'projt'), ("wo", wo, 'mix')]:
            WS[(nm, l)] = wscr(f"ws_{nm}{l}", kind)
            jobs.append((sub(src, src.ap[l]), WS[(nm, l)], kind))
    emit_W(P, jobs)
    P.barrier(); P.reset()

    def pm_dst(row, c0, n):
        if c0 == 0:
            return (pm0L.ap[row:row + 128, 0:64], pm0L)
        return (pmL.ap[row:row + 128, c0 - 64:c0 - 64 + n], pmL)

    def psc_dst(c0, n):
        if c0 == 0:
            return (psc0L.ap[0:8, 0:64], psc0L)
        return (pscL.ap[0:8, c0 - 64:c0 - 64 + n], pscL)

    def y_src(kc, c0, n):
        g, k = kc // 4, kc % 4
        if c0 == 0:
            return (Gy0.ap[g, k * 128:(k + 1) * 128, 0:64], Gy0)
        return (YG.ap[k * 256 + g * 128:k * 256 + (g + 1) * 128, c0 - 64:c0 - 64 + n], YG)

    def m_src(row, nrows, c, n):
        if c < 64:
            return (PG0.ap[row:row + nrows, c:c + n], PG0)
        h, cl = (c - 64) // NH, (c - 64) % NH
        kk, ro = row // 128, row % 128
        return (PGh[h].ap[kk * 128 + ro:kk * 128 + ro + nrows, cl:cl + n], PGh[h])

    sc_pieces = [(0, 1, (SG0.ap[:, 0:64], SG0))]
    for h in range(2):
        for q in range(NH // 1024):
            sc_pieces.append((1 + h * 64 + q * 16, 16, (SG.ap[h, :, q * 1024:(q + 1) * 1024], SG)))

    def ydst(row, c0, n):
        if c0 == 0:
            return [(y0L.ap[row:row + 128, 0:64], y0L)]
        h, cl = (c0 - 64) // NH, (c0 - 64) % NH
        return [(yL.ap[h, row:row + 128, cl:cl + n], yL)]

    def exchange_proj():
        P.barrier()
        for k in range(32):
            coll(pmL.ap[k * 128:(k + 1) * 128, :], pmL, Gall.ap[k // 16, k % 16], Gall)
        coll(pm0L.ap, pm0L, G0.ap.rearrange("r g p c -> (r g p) c"), G0)
        coll(pscL.ap, pscL, Gs.ap.rearrange("r g p c -> (r g p) c"), Gs)
        coll(psc0L.ap, psc0L, Gs0.ap.rearrange("r g p c -> (r g p) c"), Gs0)
        P.barrier()
        dsel(PG0.ap, PG0, (lambda val: G0.ap[0][ds(val, 1), :, :]), G0)
        dsel(SG0.ap, SG0, (lambda val: Gs0.ap[0][ds(val, 1), :, :]), Gs0)
        for h in range(2):
            dsel(SG.ap[h], SG, (lambda val, h=h: Gs.ap[h][ds(val, 1), :, :]), Gs)
        for h in range(2):
            for part in range(2):
                dsel(PGh[h].ap[part * 1024:(part + 1) * 1024, :], PGh[h],
                     (lambda val, h=h, part=part: Gall.ap[ds(val, 1), part * 8:(part + 1) * 8, h * 128:(h + 1) * 128, :]), Gall)
        P.reset()

    def exchange_y():
        P.barrier()
        for h in range(2):
            for kk in range(4):
                coll(yL.ap[h, kk * 128:(kk + 1) * 128, :], yL, Gy.ap[h, kk], Gy)
        coll(y0L.ap, y0L, Gy0.ap.rearrange("r p c -> (r p) c"), Gy0)
        P.barrier()
        for q in range(2):
            dsel(YG.ap[q * 512:(q + 1) * 512, :], YG, (lambda val, q=q: Gy2[ds(val, 1), q * 512:(q + 1) * 512, :]), Gy)
        P.barrier(); P.reset()

    def mixers(l):
        dd = {"src": m_src, "sc_pieces": sc_pieces, "ydst": ydst, "mp": sub(mp, mp.ap[l]), "cm": sub(cm, cm.ap[l]), "rt": rt}
        emit_M(P, dd, m_tiles, {"g": [0, 128], "w": 256, "r": 384}, ltot=L)

    base = {"pm_dst": pm_dst, "psc_dst": psc_dst, "y_src": y_src}
    dd = dict(base); dd.update({"hin": xT, "hout": hA, "ffn1_i": WS[("f1i", 0)], "ffn1_o": WS[("f1o", 0)], "wp": WS[("wp", 0)], "wpt": WS[("wpt", 0)],
                                "lnp": sub(lnp, lnp.ap[0])})
    emit_T(P, dd, ['ffn1', 'proj'], t_tiles)
    hcur, hnext = hA, hB
    for l in range(DEPTH):
        exchange_proj()
        mixers(l)
        exchange_y()
        dd = dict(base); dd.update({"hin": hcur, "w_out": WS[("wo", l)], "ffn2_i": WS[("f2i", l)], "ffn2_o": WS[("f2o", l)], "lnp": sub(lnp, lnp.ap[l + 1])})
        if l + 1 < DEPTH:
            dd.update({"hout": hnext, "ffn1_i": WS[("f1i", l + 1)], "ffn1_o": WS[("f1o", l + 1)], "wp": WS[("wp", l + 1)], "wpt": WS[("wpt", l + 1)]})
            emit_T(P, dd, ['mix', 'ffn2', 'ffn1', 'proj'], t_tiles)
            hcur, hnext = hnext, hcur
        else:
            dd["hout"] = out
            emit_T(P, dd, ['mix', 'ffn2'], t_tiles)
    P.emit()
    return nc


def wout_perm():
    rows = []
    for g in range(2):
        rows += list(range((2 * g) * 128, (2 * g + 1) * 128)) + list(range((2 * g + 1) * 128, (2 * g + 2) * 128))
        rows += list(range(512 + g * 128, 512 + (g + 1) * 128)) + list(range(768 + g * 128, 768 + (g + 1) * 128))
    return np.array(rows)


def host_inputs(z, b, r):
    f32 = np.float32
    c = np.ascontiguousarray
    tok = np.concatenate([np.zeros((48, 1024), f32), z['meta_tokens'], z['x'][b, r * NH:(r + 1) * NH]], 0)
    lnp = np.stack([lnp_pack([z['ln_g'][0, 0], z['ln_g'][0, 1], z['ln_g'][0, 2]], [z['ln_b'][0, 0], z['ln_b'][0, 1], z['ln_b'][0, 2]]),
                    lnp_pack([z['ln_g'][1, 0], z['ln_g'][0, 1], z['ln_g'][0, 2]], [z['ln_b'][1, 0], z['ln_b'][0, 1], z['ln_b'][0, 2]]),
                    lnp_pack([z['ln_g'][1, 0], z['ln_g'][1, 1], z['ln_g'][1, 2]], [z['ln_b'][1, 0], z['ln_b'][1, 1], z['ln_b'][1, 2]])], 0)
    return {"xT": c(tok.T), "lnp": lnp, "gsel": np.array([[r]], np.int32),
            "mp": np.stack([mp_pack(z, l, r) for l in range(DEPTH)], 0), "cm": np.stack([cm_pack(z, l, r) for l in range(DEPTH)], 0)}


def shared_inputs(z):
    c = np.ascontiguousarray
    cols = proj_cols()
    return {"w_ff1_in": z['w_ff1_in'], "w_ff1_out": z['w_ff1_out'], "w_ff2_in": z['w_ff2_in'], "w_ff2_out": z['w_ff2_out'],
            "wp": c(z['w_in'][:, :, cols]), "w_out": c(z['w_out'][:, wout_perm(), :]), "rt": rt_pack(L)}


_NC = {}


def kernel(x, meta_tokens, ln_g, ln_b, w_ff1_in, w_ff1_out, w_ff2_in, w_ff2_out, w_in, w_out,
           gdn_conv_w, gdn_a_log, gdn_dt_bias, gdn_norm_w, rwkv_mu, rwkv_w0, rwkv_w_up,
           rwkv_a0, rwkv_a_up, rwkv_g_up, rwkv_k_k, rwkv_k_a, rwkv_r_k, rwkv_lnx_g, rwkv_lnx_b,
           ret_norm_g, ret_norm_b):
    z = dict(x=x, meta_tokens=meta_tokens, ln_g=ln_g, ln_b=ln_b, w_ff1_in=w_ff1_in, w_ff1_out=w_ff1_out,
             w_ff2_in=w_ff2_in, w_ff2_out=w_ff2_out, w_in=w_in, w_out=w_out, gdn_conv_w=gdn_conv_w,
             gdn_a_log=gdn_a_log, gdn_dt_bias=gdn_dt_bias, gdn_norm_w=gdn_norm_w, rwkv_mu=rwkv_mu,
             rwkv_w0=rwkv_w0, rwkv_w_up=rwkv_w_up, rwkv_a0=rwkv_a0, rwkv_a_up=rwkv_a_up, rwkv_g_up=rwkv_g_up,
             rwkv_k_k=rwkv_k_k, rwkv_k_a=rwkv_k_a, rwkv_r_k=rwkv_r_k, rwkv_lnx_g=rwkv_lnx_g,
             rwkv_lnx_b=rwkv_lnx_b, ret_norm_g=ret_norm_g, ret_norm_b=ret_norm_b)
    z = {k: np.ascontiguousarray(np.asarray(v, dtype=np.float32)) for k, v in z.items()}
    B = z['x'].shape[0]
    if 'nc' not in _NC:
        _NC['nc'] = build_all()
    sh = shared_inputs(z)
    maps = []
    for core in range(8):
        m = dict(sh)
        m.update(host_inputs(z, core // 2, core % 2))
        maps.append(m)
    res = run_bass_kernel_spmd(_NC['nc'], maps, core_ids=list(range(8))).results
    out = np.zeros((B, 2 * NH, 1024), np.float32)
    for core in range(8):
        b, r = core // 2, core % 2
        out[b, r * NH:(r + 1) * NH] = res[core]["out"][:, 64:].T
    return out
```
